# Optimizing a Trainium2 kernel written in Bass

```python
import jax, jax.numpy as jnp
from jax import lax
import numpy as np

D_MODEL = 1024
BATCH = 16
SEQ = 2048
DEPTH = 4
DEC_BATCH = 32
DEC_SEQ = 64
PAST_LEN = 2048

CHUNK = 64
N_EVEN = (DEPTH + 1) // 2
N_ODD = DEPTH // 2
EPS = 1e-6
W_A = D_MODEL // 2
CONV_A_W = 3
HD_B = 64
W_B = D_MODEL // 2
H_B = W_B // HD_B
PAST_CHUNKS_B = 8
BAND_PAST = PAST_CHUNKS_B * CHUNK
BAND = BAND_PAST + CHUNK
REL_CLIP = 4 * CHUNK
HD_C = 128
H_C = D_MODEL // HD_C
W_C = H_C * HD_C
CONV_C_W = 4
IN_EVEN = 4 * W_A + 4 * W_B
IN_ODD = 4 * W_C + 2 * H_C

kernel_name = 'hybrid_streaming_conv_band_attn_gdn_step'


def rmsnorm(x, g):
    xf = x.astype(jnp.float32)
    y = xf * lax.rsqrt(jnp.mean(xf * xf, axis=-1, keepdims=True) + EPS)
    return (y * g.astype(jnp.float32)).astype(x.dtype)


def l2norm(x):
    return x * lax.rsqrt(jnp.sum(x * x, axis=-1, keepdims=True) + EPS)


def causal_dwconv(u, hist, w):
    width = w.shape[0]
    T = u.shape[1]
    up = jnp.concatenate([hist.astype(u.dtype), u], axis=1)
    out = up[:, 0:T] * w[0]
    for j in range(1, width):
        out = out + up[:, j:j + T] * w[j]
    return out, up[:, up.shape[1] - (width - 1):]


def rel_bias(table, q_pos, k_pos):
    idx = jnp.clip(q_pos[:, None] - k_pos[None, :], -REL_CLIP, REL_CLIP) + REL_CLIP
    return jnp.take(table, idx, axis=1).astype(jnp.float32)


def attend(q, k, v, bias, valid):
    s = jnp.einsum('bqhd,bkhd->bhqk', q, k, preferred_element_type=jnp.float32) * (HD_B ** -0.5) + bias[None]
    s = jnp.where(valid[None, None, None, :], s, -jnp.inf)
    p = jax.nn.softmax(s, axis=-1)
    return jnp.einsum('bhqk,bkhd->bqhd', p.astype(v.dtype), v)


def band_attn_prompt(q, k, v, table):
    Bn, T, H, Dh = q.shape
    nc = T // CHUNK
    pad = ((0, 0), (BAND_PAST, 0), (0, 0), (0, 0))
    kpad = jnp.pad(k, pad)
    vpad = jnp.pad(v, pad)
    qc = jnp.moveaxis(q.reshape(Bn, nc, CHUNK, H, Dh), 1, 0)
    k_off = jnp.arange(BAND) - BAND_PAST
    bias = rel_bias(table, jnp.arange(CHUNK), k_off)

    def one(args):
        qb, c = args
        start = c * CHUNK
        kb = lax.dynamic_slice_in_dim(kpad, start, BAND, axis=1)
        vb = lax.dynamic_slice_in_dim(vpad, start, BAND, axis=1)
        valid = (start + k_off) >= 0
        return attend(qb, kb, vb, bias, valid)

    o = lax.map(one, (qc, jnp.arange(nc)))
    o = jnp.moveaxis(o, 0, 1).reshape(Bn, T, H, Dh)
    keep = min(BAND_PAST, T)
    return o, k[:, T - keep:], v[:, T - keep:]


def band_attn_sample(q, k, v, ck, cv, table):
    T = q.shape[1]
    cb = ck.shape[1]
    kk = jnp.concatenate([ck.astype(k.dtype), k], axis=1)
    vv = jnp.concatenate([cv.astype(v.dtype), v], axis=1)
    bias = rel_bias(table, jnp.arange(T), jnp.arange(cb + T) - cb)
    valid = jnp.ones((cb + T,), dtype=bool)
    return attend(q, kk, vv, bias, valid)


def gdn_chunk(S, q, k, v, g, beta):
    L = q.shape[1]
    gc = jnp.cumsum(g, axis=1)
    gct = jnp.swapaxes(gc, 1, 2)
    incl = jnp.tril(jnp.ones((L, L), dtype=bool))
    strict = jnp.tril(jnp.ones((L, L), dtype=bool), -1)
    decay = jnp.exp(jnp.where(incl, gct[..., :, None] - gct[..., None, :], -jnp.inf))
    bt = jnp.swapaxes(beta, 1, 2)
    kk = jnp.einsum('blhd,bmhd->bhlm', k, k)
    a = jnp.where(strict, bt[..., :, None] * kk * decay, 0.0)
    tmat = a + jnp.eye(L, dtype=a.dtype)
    rhs = jnp.swapaxes(jnp.concatenate([v * beta[..., None], k * (beta * jnp.exp(gc))[..., None]], axis=-1), 1, 2)
    sol = lax.linalg.triangular_solve(tmat, rhs, left_side=True, lower=True, unit_diagonal=True)
    u = sol[..., :HD_C] - jnp.einsum('bhlk,bhkv->bhlv', sol[..., HD_C:], S)
    qk = jnp.einsum('blhd,bmhd->bhlm', q, k) * decay
    o = jnp.einsum('blhk,bhkv->blhv', q * jnp.exp(gc)[..., None], S) + jnp.einsum('bhlm,bhmv->blhv', qk, u)
    g_last = gc[:, -1]
    k_dec = k * jnp.exp(g_last[:, None, :] - gc)[..., None]
    S_new = jnp.exp(g_last)[..., None, None] * S + jnp.einsum('blhk,bhlv->bhkv', k_dec, u)
    return S_new, o


def gated_delta(q, k, v, g, beta, S0):
    Bn, T = q.shape[0], q.shape[1]
    if T <= CHUNK:
        return gdn_chunk(S0, q, k, v, g, beta)
    nc = T // CHUNK

    def split(t):
        return jnp.moveaxis(t.reshape((Bn, nc, CHUNK) + t.shape[2:]), 1, 0)

    def step(S, xs):
        return gdn_chunk(S, *xs)

    S, o = lax.scan(step, S0, (split(q), split(k), split(v), split(g), split(beta)))
    return S, jnp.moveaxis(o, 0, 1).reshape(Bn, T, H_C, HD_C)


def mixer_even(h, conv_hist, ck, cv, w_in, conv_w, table, w_out):
    Bn, T, _ = h.shape
    p = h @ w_in
    a_b, a_c, a_h, a_z, q, k, v, b_z = jnp.split(p, 8, axis=-1)
    conv_out, new_hist = causal_dwconv(a_c * a_h, conv_hist, conv_w)
    ya = a_b * conv_out * jax.nn.silu(a_z)
    q = q.reshape(Bn, T, H_B, HD_B)
    k = k.reshape(Bn, T, H_B, HD_B)
    v = v.reshape(Bn, T, H_B, HD_B)
    if ck is None:
        o, nk, nv = band_attn_prompt(q, k, v, table)
    else:
        o = band_attn_sample(q, k, v, ck, cv, table)
        nk, nv = k, v
    yb = o.reshape(Bn, T, W_B) * jax.nn.silu(b_z)
    y = jnp.concatenate([ya, yb], axis=-1) @ w_out
    return y, new_hist, nk, nv


def mixer_odd(h, conv_hist, S0, w_in, conv_w, a_log, dt_bias, onorm, w_out):
    Bn, T, _ = h.shape
    f32 = jnp.float32
    p = h @ w_in
    qkv, z, b_lg, a_lg = jnp.split(p, [3 * W_C, 4 * W_C, 4 * W_C + H_C], axis=-1)
    qkv, new_hist = causal_dwconv(qkv, conv_hist, conv_w)
    qkv = jax.nn.silu(qkv).astype(f32)
    q, k, v = [t.reshape(Bn, T, H_C, HD_C) for t in jnp.split(qkv, 3, axis=-1)]
    q = l2norm(q) * (HD_C ** -0.5)
    k = l2norm(k)
    beta = jax.nn.sigmoid(b_lg.astype(f32))
    g = -jnp.exp(a_log.astype(f32)) * jax.nn.softplus(a_lg.astype(f32) + dt_bias.astype(f32))
    S, o = gated_delta(q, k, v, g, beta, S0.astype(f32))
    o = rmsnorm(o, onorm).reshape(Bn, T, W_C).astype(h.dtype)
    y = (o * jax.nn.silu(z)) @ w_out
    return y, new_hist, S


def setup_inputs(seed: int = 0) -> dict:
    key = jax.random.key(seed)
    ks = jax.random.split(key, 20)
    cb = min(BAND_PAST, PAST_LEN)

    def nrm(k, shape, s):
        return jax.random.normal(k, shape, jnp.float32) * s

    dt = jnp.exp(jax.random.uniform(ks[18], (N_ODD, H_C), jnp.float32, np.log(1e-3), np.log(1e-1)))
    return {
        'x_prompt': nrm(ks[0], (BATCH, SEQ, D_MODEL), 1.0),
        'x_sample': nrm(ks[1], (DEC_BATCH, DEC_SEQ, D_MODEL), 1.0),
        'cache_conv_a': nrm(ks[2], (N_EVEN, DEC_BATCH, CONV_A_W - 1, W_A), 1.0),
        'cache_k_b': nrm(ks[3], (N_EVEN, DEC_BATCH, cb, H_B, HD_B), 1.0),
        'cache_v_b': nrm(ks[4], (N_EVEN, DEC_BATCH, cb, H_B, HD_B), 1.0),
        'state_conv_c': nrm(ks[5], (N_ODD, DEC_BATCH, CONV_C_W - 1, 3 * W_C), 1.0),
        'state_s_c': nrm(ks[6], (N_ODD, DEC_BATCH, H_C, HD_C, HD_C), 0.1),
        'norm_pre': 1.0 + nrm(ks[7], (DEPTH, D_MODEL), 0.1),
        'norm_post': 1.0 + nrm(ks[8], (DEPTH, D_MODEL), 0.1),
        'w_in_even': nrm(ks[9], (N_EVEN, D_MODEL, IN_EVEN), D_MODEL ** -0.5),
        'conv_w_a': nrm(ks[10], (N_EVEN, CONV_A_W, W_A), CONV_A_W ** -0.5),
        'rel_bias_b': nrm(ks[11], (N_EVEN, H_B, 2 * REL_CLIP + 1), 0.3),
        'w_out_even': nrm(ks[12], (N_EVEN, W_A + W_B, D_MODEL), (W_A + W_B) ** -0.5),
        'w_in_odd': nrm(ks[13], (N_ODD, D_MODEL, IN_ODD), D_MODEL ** -0.5),
        'conv_w_c': nrm(ks[14], (N_ODD, CONV_C_W, 3 * W_C), CONV_C_W ** -0.5),
        'a_log_c': jnp.log(jax.random.uniform(ks[15], (N_ODD, H_C), jnp.float32, 1.0, 16.0)),
        'dt_bias_c': dt + jnp.log(-jnp.expm1(-dt)),
        'out_norm_c': 1.0 + nrm(ks[16], (N_ODD, HD_C), 0.1),
        'w_out_odd': nrm(ks[17], (N_ODD, W_C, D_MODEL), W_C ** -0.5),
    }


def reference(x_prompt, x_sample, cache_conv_a, cache_k_b, cache_v_b, state_conv_c, state_s_c,
              norm_pre, norm_post, w_in_even, conv_w_a, rel_bias_b, w_out_even,
              w_in_odd, conv_w_c, a_log_c, dt_bias_c, out_norm_c, w_out_odd):
    yp, ys = x_prompt, x_sample
    ca_p, kb_p, vb_p, cc_p, sc_p = [], [], [], [], []
    ca_s, kb_s, vb_s, cc_s, sc_s = [], [], [], [], []
    for layer in range(DEPTH):
        i = layer // 2
        if layer % 2 == 0:
            wts = (w_in_even[i], conv_w_a[i], rel_bias_b[i], w_out_even[i])
            hist0 = jnp.zeros((yp.shape[0], CONV_A_W - 1, W_A), yp.dtype)
            out, hst, nk, nv = mixer_even(rmsnorm(yp, norm_pre[layer]), hist0, None, None, *wts)
            yp = yp + rmsnorm(out, norm_post[layer])
            ca_p.append(hst)
            kb_p.append(nk)
            vb_p.append(nv)
            out, hst, nk, nv = mixer_even(rmsnorm(ys, norm_pre[layer]), cache_conv_a[i], cache_k_b[i], cache_v_b[i], *wts)
            ys = ys + rmsnorm(out, norm_post[layer])
            ca_s.append(hst)
            kb_s.append(nk)
            vb_s.append(nv)
        else:
            wts = (w_in_odd[i], conv_w_c[i], a_log_c[i], dt_bias_c[i], out_norm_c[i], w_out_odd[i])
            hist0 = jnp.zeros((yp.shape[0], CONV_C_W - 1, 3 * W_C), yp.dtype)
            s0 = jnp.zeros((yp.shape[0], H_C, HD_C, HD_C), jnp.float32)
            out, hst, S = mixer_odd(rmsnorm(yp, norm_pre[layer]), hist0, s0, *wts)
            yp = yp + rmsnorm(out, norm_post[layer])
            cc_p.append(hst)
            sc_p.append(S.astype(x_prompt.dtype))
            out, hst, S = mixer_odd(rmsnorm(ys, norm_pre[layer]), state_conv_c[i], state_s_c[i], *wts)
            ys = ys + rmsnorm(out, norm_post[layer])
            cc_s.append(hst)
            sc_s.append(S.astype(state_s_c.dtype))
    return (yp, ys,
            jnp.stack(ca_p), jnp.stack(kb_p), jnp.stack(vb_p), jnp.stack(cc_p), jnp.stack(sc_p),
            jnp.stack(ca_s), jnp.stack(kb_s), jnp.stack(vb_s), jnp.stack(cc_s), jnp.stack(sc_s))
```

```python
import contextlib
import os
import numpy as np
import concourse.bass as bass
import concourse.mybir as mybir
from concourse.bass_utils import run_bass_kernel_spmd

F32 = mybir.dt.float32
BF16 = mybir.dt.bfloat16
F32R = mybir.dt.float32r
AF = mybir.ActivationFunctionType
ALU = mybir.AluOpType
AX = mybir.AxisListType

NCORES = 8
D = 1024
SEQ = 2048
NPS = 2
NSS = 4
DEC = 64
EPS = 1e-6
NEG = -30000.0


class Buf:
    __slots__ = ("name", "w", "r", "excl")

    def __init__(self, name, excl=False):
        self.name = name
        self.w = None
        self.r = []
        self.excl = excl


class Prog:
    ENGS = ("pe", "dve", "act", "pool", "sp")
    XLAT = float(os.environ.get('KXLAT', '150'))
    PECOEF = float(os.environ.get('KPECOEF', '0.4'))
    BUCKET = float(os.environ.get('KBUCKET', '2000'))

    def __init__(self, nc, stack, n_dma_sems=12, sched=True):
        self.nc = nc
        self.stack = stack
        self.sched = sched
        self.final = {e: [] for e in self.ENGS}
        self.sems = {}
        self.cnt = {}
        self.waited = {e: {} for e in self.ENGS}
        self.phase = -1
        self.dma_pool = []
        self.dma_rr = 0
        self.n_sw = 0
        self.recs = []
        self.filler = None
        self.nfill = 0
        self.GAPMIN = float(os.environ.get("KGAPMIN", "800"))
        self.FSPACE = float(os.environ.get("KFSPACE", "300"))
        self.seg_start = 0
        self.new_phase()
        for i in range(n_dma_sems):
            k = ("dma", i)
            self._mksem(k)
            self.dma_pool.append(k)

    def _mksem(self, key):
        name = "s_" + "_".join(str(x) for x in key)
        h = self.stack.enter_context(self.nc.semaphore(name))
        self.sems[key] = h
        self.cnt[key] = 0
        return h

    def new_phase(self):
        self.phase += 1
        for e in self.ENGS:
            self._mksem((e, self.phase))

    def _preds(self, eng, reads, writes):
        ps = set()
        for b in reads:
            if b.w is not None:
                ps.add(b.w)
            if b.excl:
                ps.update(d for d in b.r if self.recs[d]["eng"] != eng)
        for b in writes:
            if b.w is not None:
                ps.add(b.w)
            ps.update(b.r)
        return ps

    def _mark(self, oid, reads, writes):
        for b in writes:
            b.w = oid
            b.r = []
        for b in reads:
            if b not in writes:
                b.r.append(oid)

    @staticmethod
    def _free_elems(ap):
        n = 1
        for d in ap.shape[1:]:
            n *= d
        return n

    def _dur(self, eng, fn):
        insts = fn if isinstance(fn, list) else [fn]
        tot = 0.0
        for (name, args, kw) in insts:
            if eng == "pe":
                if name == "transpose":
                    n = self._free_elems(args[1])
                    tot += 64 + self.PECOEF * max(n, 64)
                else:
                    rhs = kw["rhs"]
                    n = self._free_elems(rhs)
                    c = 64 + self.PECOEF * max(n, 64)
                    if rhs.dtype == F32:
                        c *= 4
                    tot += c
            elif eng == "dve":
                ap = kw.get("in0", kw.get("in_", kw.get("out", args[0] if args else None)))
                n = self._free_elems(ap)
                tot += 260 + 0.66 * n * (8 if name == "reciprocal" else 1)
            elif eng == "act":
                ap = kw.get("in_", kw.get("out"))
                n = self._free_elems(ap)
                tot += 330 + 0.52 * n + (100 if kw.get("accum_out") is not None else 0)
            elif eng == "pool":
                ap = kw.get("in0", kw.get("in_", kw.get("out", args[0] if args else None)))
                n = self._free_elems(ap)
                tot += 250 + 2.0 * n
            else:
                tot += 100
        return tot

    def op(self, eng, fn, reads=(), writes=()):
        oid = len(self.recs)
        self.recs.append(dict(eng=eng, fn=fn, preds=self._preds(eng, reads, writes), kind="op",
                              dur=self._dur(eng, fn), lat=0.0, tag=",".join(b.name for b in writes)))
        self._mark(oid, reads, writes)
        return oid

    def dma(self, fn, reads=(), writes=(), queue="sp"):
        oid = len(self.recs)
        out = fn[2]["out"]
        nbytes = 128 * self._free_elems(out) * 4
        self.recs.append(dict(eng=queue, fn=fn, preds=self._preds(queue, reads, writes), kind="dma",
                              dur=(900.0 if queue == "pool" else 120.0), lat=2000.0 + nbytes / 150.0,
                              tag=",".join(b.name for b in writes)))
        self._mark(oid, reads, writes)
        return oid

    def _schedule(self, ids):
        if not self.sched:
            return ids
        recs = self.recs
        idset = set(ids)
        npred = {}
        succs = {i: [] for i in ids}
        for i in ids:
            ps = [p for p in recs[i]["preds"] if p in idset]
            npred[i] = len(ps)
            for p in ps:
                succs[p].append(i)
        import heapq
        rank = {}
        for i in reversed(ids):
            m = 0.0
            for sx in succs[i]:
                if rank[sx] > m:
                    m = rank[sx]
            rank[i] = m + recs[i]["dur"] + recs[i]["lat"]
        ready = {e: [] for e in self.ENGS}
        data_ready = {i: 0.0 for i in ids}
        for i in ids:
            if npred[i] == 0:
                heapq.heappush(ready[recs[i]["eng"]], i)
        efree = {e: 0.0 for e in self.ENGS}
        finish = {}
        order = []
        LOOK = int(os.environ.get('KLOOK', '200'))
        n_left = len(ids)
        while n_left:
            best = None
            for e in self.ENGS:
                h = ready[e]
                if not h:
                    continue
                cands = heapq.nsmallest(LOOK, h)
                for i in cands:
                    st = max(efree[e], data_ready[i])
                    key = (int(st / self.BUCKET), -rank[i], i)
                    if best is None or key < best[0]:
                        best = (key, e, i, st)
            _, e, i, st = best
            ready[e].remove(i)
            heapq.heapify(ready[e])
            r = recs[i]
            efree[e] = st + r["dur"]
            finish[i] = st + r["dur"] + r["lat"]
            r["st"] = st
            r["fin"] = finish[i]
            order.append(i)
            n_left -= 1
            for s in succs[i]:
                lat = 0.0 if recs[s]["eng"] == e and r["kind"] == "op" else self.XLAT
                data_ready[s] = max(data_ready[s], finish[i] + lat)
                npred[s] -= 1
                if npred[s] == 0:
                    heapq.heappush(ready[recs[s]["eng"]], s)
        if os.environ.get("KVERBOSE"):
            busy = {e: 0.0 for e in self.ENGS}
            for i in ids:
                busy[recs[i]["eng"]] += recs[i]["dur"]
            print("segment nops=%d makespan=%.0fus critpath=%.0fus busy(us)=%s" % (
                len(ids), max(finish.values()) / 1e3, max(rank.values()) / 1e3,
                {e: round(v / 1e3) for e, v in busy.items()}), flush=True)
            if os.environ.get("KCRIT") and len(ids) > 3000:
                i = max(ids, key=lambda q: rank[q])
                path = []
                while True:
                    path.append(i)
                    nx = [q for q in succs[i]]
                    if not nx:
                        break
                    i = max(nx, key=lambda q: rank[q])
                from collections import Counter
                cnt = Counter()
                for q in path:
                    f = recs[q]["fn"]
                    f0 = f[0] if isinstance(f, list) else f
                    cnt[(recs[q]["eng"], f0[0])] += recs[q]["dur"] + recs[q]["lat"]
                if os.environ.get("KCRIT") == "2":
                    k0 = len(path) // 2
                    for q in path[k0:k0 + 260]:
                        f = recs[q]["fn"]
                        f0 = f[0] if isinstance(f, list) else f
                        print("   ", q, recs[q]["eng"], f0[0], recs[q]["tag"], round(recs[q]["dur"]))
                print("critpath ops:", len(path), sorted(((round(v / 1e3), k) for k, v in cnt.items()), reverse=True)[:12])
        return order

    def _emit_segment(self):
        ids = list(range(self.seg_start, len(self.recs)))
        self.seg_start = len(self.recs)
        if not ids:
            return
        order = self._schedule(ids)
        recs = self.recs
        for i in order:
            r = recs[i]
            e = r["eng"]
            if r["kind"] == "dma":
                if e == "pool":
                    semkey = ("swdma", self.n_sw)
                    self.n_sw += 1
                    self._mksem(semkey)
                    r["prev"] = 0
                else:
                    semkey = self.dma_pool[self.dma_rr % len(self.dma_pool)]
                    self.dma_rr += 1
                    r["prev"] = self.cnt[semkey]
                self.cnt[semkey] += 16
                r["sem"] = (semkey, self.cnt[semkey])
                r["inc"] = 16
            else:
                k = (e, self.phase)
                self.cnt[k] += 1
                r["sem"] = (k, self.cnt[k])
                r["inc"] = 1
        import bisect
        marks = sorted((recs[i]["fin"], i) for i in order
                       if recs[i]["eng"] in ("dve", "act") and recs[i]["kind"] == "op" and "fin" in recs[i])
        mark_t = [m[0] for m in marks]
        mark_i = [m[1] for m in marks]
        pe_prev_fin = 0.0
        for i in order:
            r = recs[i]
            e = r["eng"]
            need = {}
            for p in r["preds"]:
                pr = recs[p]
                if pr["eng"] == "pe" and e == "pe" and pr["kind"] == "op" and r["kind"] == "op":
                    continue
                k, v = pr["sem"]
                if self.waited[e].get(k, 0) >= v:
                    continue
                if need.get(k, 0) < v:
                    need[k] = v
            if r["kind"] == "dma" and r["prev"] > 0:
                k = r["sem"][0]
                if self.waited[e].get(k, 0) < r["prev"] and need.get(k, 0) < r["prev"]:
                    need[k] = r["prev"]
            if e == "pe" and self.filler is not None and self.sched and "st" in r:
                gap = r["st"] - pe_prev_fin
                if gap > self.GAPMIN:
                    nmark = min(int(gap / self.FSPACE), 24)
                    for m in range(1, nmark + 1):
                        tm = pe_prev_fin + m * (gap / (nmark + 1))
                        j = bisect.bisect_right(mark_t, tm) - 1
                        if j < 0:
                            continue
                        q = recs[mark_i[j]]
                        k, v = q["sem"]
                        fw = []
                        if self.waited[e].get(k, 0) < v:
                            fw.append((k, v))
                            self.waited[e][k] = v
                        self.final[e].append((fw, self.filler, None, 0))
                        self.nfill += 1
                pe_prev_fin = r["st"] + r["dur"]
            need = {k: v for k, v in need.items() if self.waited[e].get(k, 0) < v}
            for k, v in need.items():
                self.waited[e][k] = v
            self.final[e].append((list(need.items()), r["fn"], r["sem"][0], r["inc"]))

    def barrier(self):
        self._emit_segment()
        allk = [(k, v) for k, v in self.cnt.items() if v > 0]
        for e in self.ENGS:
            waits = []
            for k, v in allk:
                if self.waited[e].get(k, 0) < v:
                    waits.append((k, v))
                    self.waited[e][k] = v
            self.final[e].append((waits, None, None, 0))

    def replay(self):
        nc = self.nc
        engmap = {"pe": nc.tensor, "dve": nc.vector, "act": nc.scalar, "pool": nc.gpsimd, "sp": nc.sync}

        def run(ename):
            eng = engmap[ename]
            for (waits, fn, k, inc) in self.final[ename]:
                for (wk, wv) in waits:
                    eng.wait_ge(self.sems[wk], wv)
                if fn is None:
                    continue
                insts = fn if isinstance(fn, list) else [fn]
                inst = None
                for (name, args, kw) in insts:
                    inst = getattr(eng, name)(*args, **kw)
                if inc:
                    inst.then_inc(self.sems[k], inc)

        with nc.Block() as block:
            @block.tensor
            def _(e):
                run("pe")

            @block.vector
            def _(e):
                run("dve")

            @block.scalar
            def _(e):
                run("act")

            @block.gpsimd
            def _(e):
                run("pool")

            @block.sync
            def _(e):
                run("sp")


import os
_STOP = float(os.environ.get("KSTOP", "0"))


class _Stop(Exception):
    pass


def stage(n):
    if _STOP and n >= _STOP:
        raise _Stop()


def I(name, *args, **kw):
    return (name, args, kw)


ARENA_WORDS_CFG = 12576


def build_program(n_layers=4, jobs=(0, 1, 2)):
    nc = bass.Bass("TRN2", target_bir_lowering=False)

    def din(name, shape):
        return nc.dram_tensor(name, list(shape), F32, kind="ExternalInput").ap()

    def dout(name, shape):
        return nc.dram_tensor(name, list(shape), F32, kind="ExternalOutput").ap()

    xp = din("xp", [NPS, SEQ, D])
    xs = din("xs", [NSS, DEC, D])
    cca = din("cca", [2, NSS, 2, 512])
    ckb = din("ckb", [2, NSS, 512, 512])
    cvb = din("cvb", [2, NSS, 512, 512])
    scc = din("scc", [2, NSS, 3, 3072])
    ssc = din("ssc", [2, NSS, 8, 128, 128])
    norm_pre = din("norm_pre", [4, D])
    norm_post = din("norm_post", [4, D])
    w_in_even = din("w_in_even", [2, D, 4096])
    conv_w_a = din("conv_w_a", [2, 3, 512])
    rel_bias = din("rel_bias", [2, 8, 513])
    w_out_even = din("w_out_even", [2, D, D])
    w_in_odd = din("w_in_odd", [2, D, 4112])
    conv_w_c = din("conv_w_c", [2, 4, 3072])
    a_log = din("a_log", [2, 8])
    dt_bias = din("dt_bias", [2, 8])
    out_norm = din("out_norm", [2, 128])
    w_out_odd = din("w_out_odd", [2, D, D])
    consts = din("consts", [128, 5, 128])

    o_yp = dout("o_yp", [NPS, SEQ, D])
    o_ys = dout("o_ys", [NSS, DEC, D])
    o_ca_p = dout("o_ca_p", [2, NPS, 2, 512])
    o_kb_p = dout("o_kb_p", [2, NPS, 512, 512])
    o_vb_p = dout("o_vb_p", [2, NPS, 512, 512])
    o_cc_p = dout("o_cc_p", [2, NPS, 3, 3072])
    o_sc_p = dout("o_sc_p", [2, NPS, 8, 128, 128])
    o_ca_s = dout("o_ca_s", [2, NSS, 2, 512])
    o_kb_s = dout("o_kb_s", [2, NSS, DEC, 512])
    o_vb_s = dout("o_vb_s", [2, NSS, DEC, 512])
    o_cc_s = dout("o_cc_s", [2, NSS, 3, 3072])
    o_sc_s = dout("o_sc_s", [2, NSS, 8, 128, 128])

    btd = nc.dram_tensor("btd", [2, 128, 8 * 5 * 128], BF16, kind="Internal").ap()
    extd = nc.dram_tensor("extd", [16, 128, 768], F32, kind="Internal").ap()

    with contextlib.ExitStack() as st:
        P = Prog(nc, st, sched=os.environ.get("KSCHED", "1") == "1")

        def sb(stack, name, shape, dt=F32):
            return stack.enter_context(nc.sbuf_tensor(name, list(shape), dt))

        def ps(name, shape, dt=F32):
            return st.enter_context(nc.psum_tensor(name, list(shape), dt))

        X = sb(st, "X", [128, 16, D])
        WIN = sb(st, "WIN", [128, 8, 4112], BF16)
        WOUT = sb(st, "WOUT", [128, 8, D], BF16)
        NEU = sb(st, "NEU", [128, 3, 512], F32R)
        IDB = sb(st, "IDB", [128, 128], BF16)
        GT = sb(st, "GT", [128, 4, 8])
        EPSB = sb(st, "EPSB", [128, 1])
        hn = sb(st, "hn", [128, D], BF16)
        hT = sb(st, "hT", [128, 8, 128], BF16)
        uT = sb(st, "uT", [128, 8, 128], BF16)
        GP = sb(st, "GP", [128, D], BF16)
        SMALL = sb(st, "SMALL", [128, 16])

        B_x = [Buf("x%d" % t) for t in range(16)]
        B_win = [Buf("win%d" % g) for g in range(9)]
        B_wout = Buf("wout")
        B_const = Buf("const")
        B_idb = Buf("idb")
        B_gt = Buf("gt")
        B_eps = Buf("eps")
        B_hn = Buf("hn")
        B_hT = Buf("hT")
        B_uT = Buf("uT")
        B_gp = Buf("gp")
        B_tmp = Buf("tmp")
        B_junk = Buf("junk")
        B_small = Buf("small")
        B_small_pre = B_small

        ARENA_WORDS = [0]
        used = [0]

        class _Arena:
            def __init__(self):
                self.off = 0
                self.t = None

            def reset(self):
                self.off = 0

            def alloc(self, stack_unused, name, shape, dt=F32):
                n = 1
                for d in shape[1:]:
                    n *= d
                words = n if dt == F32 else (n + 1) // 2
                ap = self.t[:, self.off:self.off + words]
                self.off += words
                assert self.off <= ARENA_N, ("arena overflow", name, self.off)
                if dt != F32:
                    ap = ap.bitcast(dt)
                if len(shape) > 2:
                    names = " ".join("d%d" % k for k in range(len(shape) - 1))
                    kw = {"d%d" % k: shape[k + 1] for k in range(len(shape) - 2)}
                    ap = ap.rearrange("p (%s) -> p %s" % (names, names), **kw)
                return ap

        ARENA_N = ARENA_WORDS_CFG
        arena = _Arena()
        arena.t = sb(st, "ARENA", [128, ARENA_N])

        pT = ps("pT", [128, 1024], BF16)
        pg = [ps("pg%d" % i, [128, 512]) for i in range(3)]
        pA = ps("pA", [128, 1024])
        pB = ps("pB", [128, 1024])
        B_pT = Buf("pT", True)
        B_pg = [Buf("pg%d" % i, True) for i in range(3)]
        B_pA = Buf("pA", True)
        B_pB = Buf("pB", True)
        B_pA1 = Buf("pA1", True)
        B_pB1 = Buf("pB1", True)
        BF_pA = [B_pA, B_pA1]
        BF_pB = [B_pB, B_pB1]
        pg_rr = [0]

        def next_pg():
            i = pg_rr[0] % 2
            pg_rr[0] += 1
            return pg[i], B_pg[i]

        fr_rr = [0]
        bk_rr = [0]

        def next_front():
            i = fr_rr[0] % 2
            fr_rr[0] += 1
            return pg[i], B_pg[i]

        def next_back():
            i = bk_rr[0] % 2
            bk_rr[0] += 1
            return [(pA[:, 512:1024], B_pA1), (pB[:, 512:1024], B_pB1)][i]

        arena.reset()
        CONST0 = arena.alloc(None, "CONST0", [128, 5, 128])
        P.dma(I("dma_start", out=CONST0[:], in_=consts), writes=[B_const])
        P.op("dve", I("tensor_copy", out=IDB[:], in_=CONST0[:, 0, :]), reads=[B_const], writes=[B_idb])
        P.barrier()
        P.op("pool", I("memset", EPSB[:], EPS), writes=[B_eps])
        for l4 in range(4):
            gt_src = bass.AP(norm_pre.tensor, l4 * D, [[1, 128], [128, 8]])
            P.dma(I("dma_start", out=GT[:, l4, :], in_=gt_src, allow_slow_non_contiguous=True), writes=[B_gt])

        if os.environ.get("KFILL", "1") == "1":
            nf = int(os.environ.get("KNF", "4"))
            P.filler = [I("matmul", pg[2][:, 0:128], lhsT=IDB[:, :], rhs=IDB[:, :], start=True, stop=True)
                        for _ in range(nf)]
        if n_layers >= 1:
            if True:
                arena.reset()
                ts = None
                EXT = [arena.alloc(ts, "EXT%d" % i, [128, 768]) for i in range(2)]
                BTF = [arena.alloc(ts, "BTF%d" % i, [128, 5, 128]) for i in range(2)]
                BTB = arena.alloc(ts, "BTB", [128, 8, 5, 128], BF16)
                B_ext = [Buf("ext0"), Buf("ext1")]
                B_btf = [Buf("btf0"), Buf("btf1")]
                B_btb = Buf("btb")
                B_extd = [Buf("extd%d" % i) for i in range(16)]
                for li in range(2):
                    for h in range(8):
                        idx = li * 8 + h
                        e = idx % 2
                        ext, bext, btf, bbtf = EXT[e], B_ext[e], BTF[e], B_btf[e]
                        P.dma(I("dma_start",
                            out=ext[:, 0:384], in_=rel_bias[li, h:h + 1, 129:513].to_broadcast([128, 384])),
                            writes=[bext])
                        P.op("dve", I("tensor_copy",
                            out=ext[:, 384:768], in_=ext[:, 383:384].to_broadcast([128, 384])),
                            reads=[bext], writes=[bext])
                        P.dma(I("dma_start", out=extd[idx], in_=ext[:]),
                              reads=[bext], writes=[B_extd[idx]])
                        src = bass.AP(extd.tensor, idx * 128 * 768 + 127, [[767, 128], [128, 5], [1, 128]])
                        P.dma(I("dma_start", out=btf[:], in_=src),
                              reads=[B_extd[idx]], writes=[bbtf])
                        P.op("pool", I("memset", btf[64:128, 0, 0:64], NEG),
                             writes=[bbtf])
                        P.op("pool", I("memset", btf[0:64, 4, 64:128], NEG),
                             writes=[bbtf])
                        P.op("act", I("copy", out=BTB[:, h, :, :], in_=btf[:]),
                             reads=[bbtf], writes=[B_btb])
                    P.dma(I("dma_start",
                        out=btd[li], in_=BTB[:].rearrange("p a b c -> p (a b c)")), reads=[B_btb], writes=[Buf("btd")])
                P.barrier()

        def load_x(job):
            if job < 2:
                for t in range(16):
                    P.dma(I("dma_start", out=X[:, t, :], in_=xp[job, t * 128:(t + 1) * 128, :]),
                          writes=[B_x[t]])
            else:
                for t in range(NSS):
                    P.dma(I("dma_start", out=X[0:DEC, t, :], in_=xs[t]), writes=[B_x[t]])

        def store_x(job, t, NT):
            if job < 2:
                P.dma(I("dma_start", out=o_yp[job, t * 128:(t + 1) * 128, :], in_=X[0:NT, t, :]),
                      reads=[B_x[t]])
            else:
                P.dma(I("dma_start", out=o_ys[t], in_=X[0:NT, t, :]), reads=[B_x[t]])

        def load_weights(w_in, ncols, w_out, layer):
            wv = w_in.rearrange("(kc p) n -> p kc n", p=128)
            c0 = 0
            while c0 < ncols:
                c1 = min(ncols, c0 + 2048)
                gs = list(range(c0 // 512, (c1 + 511) // 512))
                P.dma(I("dma_start", out=WIN[:, :, c0:c1], in_=wv[:, :, c0:c1], max_dma_last_dim=8192),
                      writes=[B_win[g] for g in gs], queue="pool")
                c0 = c1
            wo = w_out.rearrange("(kc p) n -> p kc n", p=128)
            P.dma(I("dma_start", out=WOUT[:], in_=wo, max_dma_last_dim=8192), writes=[B_wout], queue="pool")

        def load_gp(layer, tmp, btmp):
            P.dma(I("dma_start", out=tmp, in_=norm_post[layer:layer + 1, :].to_broadcast([128, D])), writes=btmp)
            P.op("act", I("copy", out=GP[:], in_=tmp), reads=btmp, writes=[B_gp])

        B_small2 = Buf("small_post")

        def rstd(NT, src_ap, src_bufs, col, scale, junk=None, bjunk=None):
            B_small = B_small_pre if col == 0 else B_small2
            if junk is None:
                junk, bjunk = hn, B_hn
            P.op("act", I("activation", out=junk[0:NT, :], in_=src_ap, func=AF.Square,
                          accum_out=SMALL[0:NT, col:col + 1]), reads=src_bufs, writes=[bjunk, B_small])
            P.op("act", I("activation", out=SMALL[0:NT, col + 1:col + 2], in_=SMALL[0:NT, col:col + 1],
                          func=AF.Ln, bias=EPSB[0:NT, :], scale=scale), reads=[B_small, B_eps], writes=[B_small])
            P.op("act", I("activation", out=SMALL[0:NT, col + 1:col + 2], in_=SMALL[0:NT, col + 1:col + 2],
                          func=AF.Exp, scale=-0.5), reads=[B_small], writes=[B_small])

        def pre_norm(t, NT, layer):
            rstd(NT, X[0:NT, t, :], [B_x[t]], 0, 1.0 / D)
            P.op("dve", I("tensor_scalar", out=hn[0:NT, :], in0=X[0:NT, t, :],
                                                        scalar1=SMALL[0:NT, 1:2], scalar2=None, op0=ALU.mult),
                 reads=[B_x[t], B_small], writes=[B_hn])
            pTv = pT[:].rearrange("p (a b) -> p a b", a=8)
            P.op("pe", [I("transpose", pTv[:, kc, 0:NT], hn[0:NT, kc * 128:(kc + 1) * 128],
                                                                  IDB[0:NT, 0:NT]) for kc in range(8)],
                 reads=[B_hn, B_idb], writes=[B_pT])
            P.op("dve", I("tensor_tensor",
                out=hT[:, :, 0:NT], in0=pTv[:, :, 0:NT],
                in1=GT[:, layer, :].unsqueeze(2).to_broadcast([128, 8, NT]), op=ALU.mult),
                reads=[B_pT, B_gt], writes=[B_hT])

        def fm_group(c0, NT, g, nft=4, bank_fn=None):
            bank, bbank = (bank_fn or next_pg)()
            bv = bank[:].rearrange("p (a b) -> p a b", a=4)
            fns = []
            for ft in range(nft):
                for kc in range(8):
                    fns.append(I("matmul",
                        bv[:, ft, 0:NT], lhsT=WIN[:, kc, c0 + ft * 128:c0 + (ft + 1) * 128], rhs=hT[:, kc, 0:NT],
                        start=(kc == 0), stop=(kc == 7)))
            P.op("pe", fns, reads=[B_hT, B_win[g]], writes=[bbank])
            return bv, bbank

        def tm_group(c0, ncol, NT, g, bank_fn=None):
            bank, bbank = (bank_fn or next_pg)()
            fns = []
            for kc in range(8):
                fns.append(I("matmul",
                    bank[0:NT, 0:ncol], lhsT=hT[:, kc, 0:NT], rhs=WIN[:, kc, c0:c0 + ncol],
                    start=(kc == 0), stop=(kc == 7)))
            P.op("pe", fns, reads=[B_hT, B_win[g]], writes=[bbank])
            return bank, bbank

        def out_proj_post(t, NT, layer, last, uT=uT, B_uT=B_uT, junk=None, bjunk=None, Y=None, BY=None):
            if Y is None:
                Y, BY = pA, BF_pA
            fns = []
            for nb in range(2):
                for kc in range(8):
                    fns.append(I("matmul",
                        Y[0:NT, nb * 512:(nb + 1) * 512], lhsT=uT[:, kc, 0:NT], rhs=WOUT[:, kc, nb * 512:(nb + 1) * 512],
                        start=(kc == 0), stop=(kc == 7)))
            P.op("pe", fns, reads=[B_uT, B_wout], writes=BY)
            rstd(NT, Y[0:NT, :], BY, 2, 1.0 / D, junk, bjunk)
            P.op("dve", I("scalar_tensor_tensor",
                out=Y[0:NT, :], in0=Y[0:NT, :], scalar=SMALL[0:NT, 3:4], in1=GP[0:NT, :],
                op0=ALU.mult, op1=ALU.mult), reads=BY + [B_small2, B_gp], writes=BY)
            P.op("dve", I("tensor_tensor", out=X[0:NT, t, :], in0=X[0:NT, t, :], in1=Y[0:NT, :],
                          op=ALU.add), reads=[B_x[t]] + BY, writes=[B_x[t]])

        def interleave(front_gens, back_gens):
            n = len(back_gens)
            for _ in front_gens[0]:
                pass
            for t in range(n):
                b = back_gens[t]
                f = front_gens[t + 1] if t + 1 < n else iter(())
                bdone = fdone = False
                while not (bdone and fdone):
                    if not bdone:
                        try:
                            next(b)
                        except StopIteration:
                            bdone = True
                    if not fdone:
                        try:
                            next(f)
                        except StopIteration:
                            fdone = True

        def even_layer(job, layer, last):
            i = layer // 2
            prompt = job < 2
            NT = 128 if prompt else DEC
            ntiles = 16 if prompt else NSS
            load_weights(w_in_even[i], 4096, w_out_even[i], layer)
            arena.reset()
            ls = None
            sb = arena.alloc
            KT = sb(ls, "KT", [128, 4, 6 * 128], BF16)
            V = sb(ls, "V", [128, 6, 8, 65], BF16)
            BT = sb(ls, "BT", [128, 8, 5, 128], BF16)
            CW = sb(ls, "CW", [128, 4, 3])
            U = sb(ls, "U", [128, 4, 130])
            TC = sb(ls, "TC", [128, 8, 128])
            TAC, CV = TC[:, 0:4, :], TC[:, 4:8, :]
            _tmpgp = TC.rearrange("p a b -> p (a b)")
            SZ = sb(ls, "SZ", [128, 4, 128])
            E = [sb(ls, "E%d" % k, [128, 5, 128]) for k in range(2)]
            PTb = [sb(ls, "PTb%d" % k, [128, 5, 128], BF16) for k in range(2)]
            ON = sb(ls, "ON", [128, 8, 64], BF16)
            REC = sb(ls, "REC", [128, 8])
            nbuf = 2 if prompt else 1
            QTs = [sb(ls, "QT%d" % k, [128, 4, 128], BF16) for k in range(nbuf)]
            SZBs = [sb(ls, "SZB%d" % k, [128, 4, 128], BF16) for k in range(nbuf)]
            uTs = [uT] + [sb(ls, "uT%d" % k, [128, 8, 128], BF16) for k in range(1, nbuf)]
            B_uTs = [B_uT] + [Buf("uT%d" % k) for k in range(1, nbuf)]
            if prompt:
                JK = NEU[:, 0:2, :].rearrange("p a b -> p (a b)")
                B_jk = Buf("jk")
            else:
                JK, B_jk = hn, B_hn
                KC = sb(ls, "KC", [128, 4, 512], BF16)
                VC = TC.rearrange("p a b -> p (a b)").bitcast(BF16).rearrange("p (j c) -> p j c", j=4)
            B_kt = [Buf("kt%d" % k) for k in range(6)]
            B_v = [Buf("v%d" % k) for k in range(6)]
            B_bt, B_cw, B_u, B_tac, B_cv, B_sz = (Buf(n) for n in ("bt", "cw", "u", "tac", "cv", "sz"))
            load_gp(layer, _tmpgp, [B_tac, B_cv])
            B_qts = [Buf("qt%d" % k) for k in range(nbuf)]
            B_szbs = [Buf("szb%d" % k) for k in range(nbuf)]
            B_e = [Buf("e0"), Buf("e1")]
            B_ptb = [Buf("ptb0"), Buf("ptb1")]
            B_on, B_rec, B_kc = Buf("on"), Buf("rec"), Buf("kc")
            B_kvo = B_e
            KVO = [E[k][:, 0:4, :].rearrange("p a b -> p (a b)") for k in range(2)]
            pTv = pT[:].rearrange("p (a b) -> p a b", a=8)

            P.dma(I("dma_start", out=BT[:].rearrange("p a b c -> p (a b c)"), in_=btd[i]), writes=[B_bt])
            for j in range(3):
                cw_src = bass.AP(conv_w_a.tensor, (i * 3 + j) * 512, [[1, 128], [128, 4]])
                P.dma(I("dma_start", out=CW[:, :, j], in_=cw_src, allow_slow_non_contiguous=True), writes=[B_cw])
            P.op("pool", I("memset", V[:], 1.0), writes=B_v)
            P.op("pool", I("memset", KT[:], 0.0), writes=B_kt)
            P.op("pool", I("memset", U[:], 0.0), writes=[B_u])

            def tile_cfg(t):
                if prompt:
                    ns = min(5, t + 1)
                    return t % 6, ns, [(t - s) % 6 for s in range(ns)]
                return 4, 5, [4 - s for s in range(5)]

            def front(t):
                out_tile = (not prompt) or t >= 12
                kslot_new, ns, kslots = tile_cfg(t)
                QT, B_qt = QTs[t % nbuf], B_qts[t % nbuf]
                SZB, B_szb = SZBs[t % nbuf], B_szbs[t % nbuf]
                uTt, B_uTt = uTs[t % nbuf], B_uTs[t % nbuf]
                if not prompt:
                    for j in range(2):
                        hsrc = bass.AP(cca.tensor, ((i * NSS + t) * 2 + j) * 512, [[1, 128], [128, 4]])
                        P.dma(I("dma_start", out=U[:, :, j], in_=hsrc, allow_slow_non_contiguous=True), writes=[B_u])
                    kcv = ckb[i, t].rearrange("(j p) c -> p j c", p=128)
                    P.dma(I("dma_start", out=KC[:], in_=kcv, max_dma_last_dim=2048), writes=[B_kc], queue="pool")
                    vcv = cvb[i, t].rearrange("(j p) c -> p j c", p=128)
                    P.dma(I("dma_start", out=VC[:], in_=vcv, max_dma_last_dim=2048), writes=[B_tac, B_cv], queue="pool")
                    P.op("dve", I("tensor_copy", out=V[:, 0:4, :, 0:64],
                                  in_=VC[:].rearrange("p j (h d) -> p j h d", h=8)), reads=[B_tac, B_cv], writes=B_v[0:4])
                    for r in range(2):
                        fns = []
                        for jj in range(2):
                            for hp in range(4):
                                j = 2 * r + jj
                                fns.append(I("transpose", pTv[:, jj * 4 + hp, :], KC[:, j, hp * 128:(hp + 1) * 128],
                                             IDB[:]))
                        P.op("pe", fns, reads=[B_kc, B_idb], writes=[B_pT])
                        for jj in range(2):
                            j = 2 * r + jj
                            P.op("act", I("copy", out=KT[:, :, j * 128:(j + 1) * 128], in_=pTv[:, jj * 4:jj * 4 + 4, :]),
                                 reads=[B_pT], writes=[B_kt[j]])
                pre_norm(t, NT, layer)
                yield
                bv, bb = fm_group(1536, NT, 3)
                P.op("act", I("activation", out=SZ[:, :, 0:NT], in_=bv[:, :, 0:NT], func=AF.Silu),
                     reads=[bb], writes=[B_sz])
                bv, bb = fm_group(3584, NT, 7)
                P.op("act", I("activation", out=SZB[:, :, 0:NT], in_=bv[:, :, 0:NT], func=AF.Silu),
                     reads=[bb], writes=[B_szb])
                yield
                bv, bb = fm_group(512, NT, 1)
                P.op("act", I("copy", out=TAC[:, :, 0:NT], in_=bv[:, :, 0:NT]), reads=[bb], writes=[B_tac])
                yield
                bv, bb = fm_group(1024, NT, 2)
                P.op("dve", I("tensor_tensor", out=U[:, :, 2:2 + NT], in0=bv[:, :, 0:NT], in1=TAC[:, :, 0:NT],
                              op=ALU.mult), reads=[bb, B_tac], writes=[B_u])
                for ct in range(4):
                    P.op("dve", I("tensor_scalar", out=CV[:, ct, 0:NT], in0=U[:, ct, 0:NT], scalar1=CW[:, ct, 0:1],
                                  scalar2=None, op0=ALU.mult), reads=[B_u, B_cw], writes=[B_cv])
                    for j in (1, 2):
                        P.op("dve", I("scalar_tensor_tensor", out=CV[:, ct, 0:NT], in0=U[:, ct, j:j + NT],
                                      scalar=CW[:, ct, j:j + 1], in1=CV[:, ct, 0:NT], op0=ALU.mult, op1=ALU.add),
                             reads=[B_u, B_cw, B_cv], writes=[B_cv])
                if t == ntiles - 1 or not prompt:
                    for j in range(2):
                        if prompt:
                            dst = bass.AP(o_ca_p.tensor, ((i * NPS + job) * 2 + j) * 512, [[1, 128], [128, 4]])
                        else:
                            dst = bass.AP(o_ca_s.tensor, ((i * NSS + t) * 2 + j) * 512, [[1, 128], [128, 4]])
                        P.dma(I("dma_start", out=dst, in_=U[:, :, NT + j], allow_slow_non_contiguous=True), reads=[B_u])
                if prompt and t < ntiles - 1:
                    P.op("pool", I("tensor_copy", out=U[:, :, 0:2], in_=U[:, :, NT:NT + 2]), reads=[B_u], writes=[B_u])
                yield
                bv, bb = fm_group(0, NT, 0)
                P.op("dve", I("tensor_tensor", out=CV[:, :, 0:NT], in0=bv[:, :, 0:NT], in1=CV[:, :, 0:NT], op=ALU.mult),
                     reads=[bb, B_cv], writes=[B_cv])
                P.op("pool", I("tensor_tensor", out=uTt[:, 0:4, 0:NT], in0=CV[:, :, 0:NT], in1=SZ[:, :, 0:NT],
                               op=ALU.mult), reads=[B_cv, B_sz], writes=[B_uTt])
                yield
                bv, bb = fm_group(2048, NT, 4)
                P.op("act", I("activation", out=QT[:, :, 0:NT], in_=bv[:, :, 0:NT], func=AF.Copy, scale=0.125),
                     reads=[bb], writes=[B_qt])
                yield
                bv, bb = fm_group(2560, NT, 5)
                P.op("dve", I("tensor_copy", out=KT[:, :, kslot_new * 128:kslot_new * 128 + NT], in_=bv[:, :, 0:NT]),
                     reads=[bb], writes=[B_kt[kslot_new]])
                yield
                bank, bb = tm_group(3072, 512, NT, 6)
                P.op("dve", I("tensor_copy", out=V[0:NT, kslot_new, :, 0:64],
                              in_=bank[0:NT, :].rearrange("p (h d) -> p h d", h=8)), reads=[bb], writes=[B_v[kslot_new]])
                if out_tile:
                    P.op("act", I("copy", out=KVO[0][0:NT, :], in_=bank[0:NT, :]), reads=[bb], writes=[B_kvo[0]])
                    dst = o_vb_p[i, job, (t - 12) * 128:(t - 11) * 128, :] if prompt else o_vb_s[i, t]
                    P.dma(I("dma_start", out=dst, in_=KVO[0][0:NT, :]), reads=[B_kvo[0]])
                    yield
                    bank, bb = tm_group(2560, 512, NT, 5)
                    P.op("act", I("copy", out=KVO[1][0:NT, :], in_=bank[0:NT, :]), reads=[bb], writes=[B_kvo[1]])
                    dst = o_kb_p[i, job, (t - 12) * 128:(t - 11) * 128, :] if prompt else o_kb_s[i, t]
                    P.dma(I("dma_start", out=dst, in_=KVO[1][0:NT, :]), reads=[B_kvo[1]])
                yield

            def back(t):
                kslot_new, ns, kslots = tile_cfg(t)
                QT, B_qt = QTs[t % nbuf], B_qts[t % nbuf]
                SZB, B_szb = SZBs[t % nbuf], B_szbs[t % nbuf]
                uTt, B_uTt = uTs[t % nbuf], B_uTs[t % nbuf]
                stv = pA[:].rearrange("p (a b) -> p a b", a=8)
                povs = [pB[:, 0:260].rearrange("p (a b) -> p a b", a=4),
                        pB[:, 512:772].rearrange("p (a b) -> p a b", a=4)]

                def emit_pv(h):
                    hl = h % 4
                    pov = povs[h // 4]
                    fns = []
                    for s in range(ns):
                        fns.append(I("matmul", pov[0:NT, hl, :], lhsT=PTb[h % 2][:, s, 0:NT], rhs=V[:, kslots[s], h, :],
                                     start=(s == 0), stop=(s == ns - 1)))
                    P.op("pe", fns, reads=[B_ptb[h % 2]] + [B_v[k] for k in kslots], writes=BF_pB)

                for h in range(8):
                    hp, po = h // 2, (h % 2) * 64
                    fns = []
                    for s in range(ns):
                        ks = kslots[s]
                        fns.append(I("matmul", stv[:, s, 0:NT], lhsT=KT[po:po + 64, hp, ks * 128:(ks + 1) * 128],
                                     rhs=QT[po:po + 64, hp, 0:NT], start=True, stop=True))
                    P.op("pe", fns, reads=[B_qt] + [B_kt[k] for k in kslots], writes=BF_pA)
                    P.op("dve", I("tensor_tensor", out=E[h % 2][:, 0:ns, 0:NT], in0=stv[:, 0:ns, 0:NT],
                                  in1=BT[:, h, 0:ns, 0:NT], op=ALU.add), reads=BF_pA + [B_bt], writes=[B_e[h % 2]])
                    P.op("act", I("activation", out=PTb[h % 2][:, 0:ns, 0:NT], in_=E[h % 2][:, 0:ns, 0:NT], func=AF.Exp),
                         reads=[B_e[h % 2]], writes=[B_ptb[h % 2]])
                    if h > 0:
                        emit_pv(h - 1)
                    yield
                emit_pv(7)
                for half in range(2):
                    pov = povs[half]
                    P.op("dve", I("reciprocal", out=REC[0:NT, half * 4:half * 4 + 4], in_=pov[0:NT, :, 64]),
                         reads=BF_pB, writes=[B_rec])
                    P.op("dve", I("tensor_tensor", out=ON[0:NT, half * 4:half * 4 + 4, :], in0=pov[0:NT, :, 0:64],
                                  in1=REC[0:NT, half * 4:half * 4 + 4].unsqueeze(2).to_broadcast([NT, 4, 64]),
                                  op=ALU.mult), reads=BF_pB + [B_rec], writes=[B_on])
                onf = ON[:].rearrange("p a b -> p (a b)")
                P.op("pe", [I("transpose", pTv[:, ft, 0:NT], onf[0:NT, ft * 128:(ft + 1) * 128], IDB[0:NT, 0:NT])
                            for ft in range(4)], reads=[B_on, B_idb], writes=[B_pT])
                P.op("dve", I("tensor_tensor", out=uTt[:, 4:8, 0:NT], in0=pTv[:, 0:4, 0:NT], in1=SZB[:, :, 0:NT],
                              op=ALU.mult), reads=[B_pT, B_szb], writes=[B_uTt])
                yield
                out_proj_post(t, NT, layer, last, uTt, B_uTt, JK, B_jk)
                if last:
                    store_x(job, t, NT)
                yield

            if prompt:
                interleave([front(t) for t in range(ntiles)], [back(t) for t in range(ntiles)])
            else:
                for t in range(ntiles):
                    for _ in front(t):
                        pass
                    for _ in back(t):
                        pass
            P.barrier()

        def odd_layer(job, layer, last):
            i = layer // 2
            prompt = job < 2
            NT = 128 if prompt else DEC
            nch = NT // 64
            ntiles = 16 if prompt else NSS
            load_weights(w_in_odd[i], 4112, w_out_odd[i], layer)
            arena.reset()
            al = arena.alloc
            CONST = al(None, "CONST", [128, 5, 128])
            P.dma(I("dma_start", out=CONST[:], in_=consts), writes=[B_const])
            S = al(None, "S", [128, 8, 128])
            SB = al(None, "SB", [128, 8, 128], BF16)
            RH = al(None, "RH", [128, 24, 3])
            Cq = al(None, "Cq", [128, 8, 128], BF16)
            EG = al(None, "EGC", [128, 8, 128])
            Ckv = EG.rearrange("p a b -> p (a b)").bitcast(BF16).rearrange("p (a b) -> p a b", a=16)
            CWC = al(None, "CWC", [128, 24, 4])
            SM = al(None, "SM", [128, 16, 8])
            DTB = al(None, "DTB", [128, 8])
            NEGA = al(None, "NEGA", [128, 8])
            ONR = al(None, "ONR", [128, 2])
            ONESB = al(None, "ONESB", [128, 2], BF16)
            SQZ = al(None, "SQZ", [128, 1024])
            SQ = SQZ.bitcast(BF16).rearrange("p (a b) -> p a b", a=16)
            ZS = SQZ
            GO = al(None, "GO", [128, 8, 128])
            KN = al(None, "KN", [128, 8, 128], BF16)
            VB = al(None, "VB", [128, 8, 128], BF16)
            KNT = al(None, "KNT", [128, 8, 128], BF16)
            RS = [al(None, "RS%d" % k, [128, 2, 131]) for k in range(2)]
            ACC = [al(None, "ACC%d" % k, [128, 2, 128]) for k in range(2)]
            EGL = al(None, "EGL", [128, 8, 2])
            B_egl = Buf("egl")
            D1 = al(None, "D1", [128, 4, 128])
            D2 = al(None, "D2", [128, 4, 128])
            Xp_w, XT_w, PTm_w = (NEU[:, k, :].rearrange("p (a b) -> p a b", a=4) for k in range(3))
            Xp, XT, PTm = (a.bitcast(F32) for a in (Xp_w, XT_w, PTm_w))
            SOLV = al(None, "SOLV", [128, 4, 128])
            PTB = al(None, "PTB", [128, 4, 128], BF16)
            QKDT = al(None, "QKDT", [128, 4, 128], BF16)
            RHSK = al(None, "RHSK", [128, 4, 128], BF16)
            SKT = al(None, "SKT", [128, 4, 128], BF16)
            QST = al(None, "QST", [128, 4, 128], BF16)
            KDEC = al(None, "KDEC", [128, 4, 128], BF16)
            UU = al(None, "UU", [128, 4, 128], BF16)
            OGZ = al(None, "OGZ", [128, D], BF16)
            B_ogz = Buf("ogz")
            NM = 128
            B_hg = Buf("hgscratch")
            B_s, B_sb, B_rh, B_cq, B_ckv, B_cwc, B_sm, B_par, B_sqz, B_go, B_kn, B_vb, B_knt = (
                Buf(n) for n in ("s", "sb", "rh", "cq", "ckv", "cwc", "sm", "par", "sqz", "go", "kn", "vb", "knt"))
            B_rs = [Buf("rs0"), Buf("rs1")]
            B_acc = [Buf("acc0"), Buf("acc1")]
            B_d1, B_d2, B_xp, B_xt, B_ptm, B_solv, B_ptb, B_qkdt, B_rhsk, B_skt, B_qst, B_kdec, B_uu = (
                Buf(n) for n in ("d1", "d2", "xp", "xt", "ptm", "solv", "ptb", "qkdt", "rhsk", "skt", "qst", "kdec", "uu"))
            hg_bufs = [B_d1, B_d2, B_xp, B_xt, B_ptm, B_solv, B_ptb, B_qkdt, B_rhsk, B_skt, B_qst, B_kdec, B_uu]
            conv_bufs = B_rs + B_acc
            load_gp(layer, GO.rearrange("p a b -> p (a b)"), [B_go])
            if not prompt:
                for k3 in range(3):
                    P.op("dve", I("tensor_scalar", out=NEU[:, k3, :], in0=GP[:, 0:512], scalar1=0.0, scalar2=None,
                                  op0=ALU.mult), reads=[B_gp], writes=[B_xp, B_xt, B_ptm])
            c_G, c_BETA, c_GC, c_DL, c_BE, c_RQ, c_RK, c_OSS, c_FAC, c_T1, c_GA, c_NGC = range(12)
            IDF = CONST[:, 0, :]
            ONESF = CONST[:, 1, :]
            MLS = CONST[:, 3, :]
            MUI = CONST[:, 4, :]

            for j in range(4):
                src = bass.AP(conv_w_c.tensor, (i * 4 + j) * 3072, [[1, 128], [128, 24]])
                P.dma(I("dma_start", out=CWC[:, :, j], in_=src, allow_slow_non_contiguous=True), writes=[B_cwc])
            P.dma(I("dma_start", out=DTB[:], in_=dt_bias[i:i + 1, :].to_broadcast([128, 8])), writes=[B_par])
            P.dma(I("dma_start", out=NEGA[:], in_=a_log[i:i + 1, :].to_broadcast([128, 8])), writes=[B_par])
            onr_src = bass.AP(out_norm.tensor, i * 128, [[1, 128], [1, 1]])
            P.dma(I("dma_start", out=ONR[:, 0:1], in_=onr_src), writes=[B_par])
            P.op("dve", I("tensor_scalar", out=WOUT[:].rearrange("p a b -> p (a b)"),
                          in0=WOUT[:].rearrange("p a b -> p (a b)"), scalar1=ONR[:, 0:1], scalar2=None, op0=ALU.mult),
                 reads=[B_wout, B_par], writes=[B_wout])
            P.op("act", I("activation", out=NEGA[:], in_=NEGA[:], func=AF.Exp), reads=[B_par], writes=[B_par])
            P.op("dve", I("tensor_scalar", out=NEGA[:], in0=NEGA[:], scalar1=-1.0, scalar2=None, op0=ALU.mult),
                 reads=[B_par], writes=[B_par])
            P.op("pool", I("memset", ONESB[:], 1.0), writes=[B_par])
            P.op("pool", I("memset", UU[:], 0.0), writes=[B_uu])
            if prompt:
                P.op("pool", I("memset", S[:], 0.0), writes=[B_s])
                P.op("pool", I("memset", SB[:], 0.0), writes=[B_sb])
                P.op("pool", I("memset", RH[:], 0.0), writes=[B_rh])
            stage(12)

            for t in range(ntiles):
                seq = job if prompt else t
                if not prompt:
                    for j in range(3):
                        src = bass.AP(scc.tensor, ((i * NSS + t) * 3 + j) * 3072, [[1, 128], [128, 24]])
                        P.dma(I("dma_start", out=RH[:, :, j], in_=src, allow_slow_non_contiguous=True), writes=[B_rh])
                    ssrc = bass.AP(ssc.tensor, (i * NSS + t) * 8 * 128 * 128, [[128, 128], [16384, 8], [1, 128]])
                    P.dma(I("dma_start", out=S[:], in_=ssrc), writes=[B_s])
                    P.op("act", I("copy", out=SB[:], in_=S[:]), reads=[B_s], writes=[B_sb])
                pre_norm(t, NT, layer)
                stage(13)
                for g in range(6):
                    bv, bb = fm_group(g * 512, NT, g, bank_fn=next_front)
                    for hf in range(2):
                        k2 = (2 * g + hf) % 2
                        rs, brs = RS[k2], B_rs[k2]
                        acc, bacc = ACC[k2], B_acc[k2]
                        f0 = g * 4 + hf * 2
                        P.op("pool", I("tensor_copy", out=rs[:, :, 0:3], in_=RH[:, f0:f0 + 2, :]),
                             reads=[B_rh], writes=[brs])
                        P.op("act", I("copy", out=rs[:, :, 3:3 + NT], in_=bv[:, hf * 2:hf * 2 + 2, 0:NT]),
                             reads=[bb], writes=[brs])
                        P.op("pool", I("tensor_copy", out=RH[:, f0:f0 + 2, :], in_=rs[:, :, NT:NT + 3]),
                             reads=[brs], writes=[B_rh])
                        for f2 in range(2):
                            ft = f0 + f2
                            P.op("act", I("activation", out=acc[:, f2, 0:NT], in_=rs[:, f2, 0:NT], func=AF.Identity,
                                          scale=CWC[:, ft, 0:1]), reads=[brs, B_cwc], writes=[bacc])
                            for j in (1, 2, 3):
                                P.op("dve", I("scalar_tensor_tensor", out=acc[:, f2, 0:NT], in0=rs[:, f2, j:j + NT],
                                              scalar=CWC[:, ft, j:j + 1], in1=acc[:, f2, 0:NT],
                                              op0=ALU.mult, op1=ALU.add), reads=[brs, B_cwc, bacc], writes=[bacc])
                        if f0 < 8:
                            dst, bd = Cq[:, f0:f0 + 2, 0:NT], B_cq
                        else:
                            dst, bd = Ckv[:, f0 - 8:f0 - 6, 0:NT], B_ckv
                        P.op("act", I("activation", out=dst, in_=acc[:, :, 0:NT], func=AF.Silu), reads=[bacc], writes=[bd])
                if t == ntiles - 1 or not prompt:
                    for j in range(3):
                        if prompt:
                            dst = bass.AP(o_cc_p.tensor, ((i * NPS + job) * 3 + j) * 3072, [[1, 128], [128, 24]])
                        else:
                            dst = bass.AP(o_cc_s.tensor, ((i * NSS + t) * 3 + j) * 3072, [[1, 128], [128, 24]])
                        P.dma(I("dma_start", out=dst, in_=RH[:, :, j], allow_slow_non_contiguous=True), reads=[B_rh])
                stage(14)
                bank, bb = tm_group(4096, 16, NT, 8, bank_fn=next_back)
                P.op("dve", I("tensor_tensor", out=SM[0:NT, c_GA, :], in0=bank[0:NT, 8:16], in1=DTB[0:NT, :], op=ALU.add),
                     reads=[bb, B_par], writes=[B_sm])
                P.op("act", I("activation", out=SM[0:NT, c_BETA, :], in_=bank[0:NT, 0:8], func=AF.Exp, scale=-1.0),
                     reads=[bb], writes=[B_sm])
                P.op("act", I("activation", out=SM[0:NT, c_BETA, :], in_=SM[0:NT, c_BETA, :], func=AF.Ln,
                              bias=CONST[0:NT, 1, 0:1], scale=1.0), reads=[B_sm, B_const], writes=[B_sm])
                P.op("act", I("activation", out=SM[0:NT, c_BETA, :], in_=SM[0:NT, c_BETA, :], func=AF.Exp, scale=-1.0),
                     reads=[B_sm], writes=[B_sm])
                P.op("act", I("activation", out=SM[0:NT, c_GA, :], in_=SM[0:NT, c_GA, :], func=AF.Exp),
                     reads=[B_sm], writes=[B_sm])
                P.op("act", I("activation", out=SM[0:NT, c_GA, :], in_=SM[0:NT, c_GA, :], func=AF.Ln,
                              bias=CONST[0:NT, 1, 0:1], scale=1.0), reads=[B_sm, B_const], writes=[B_sm])
                P.op("dve", I("tensor_tensor", out=SM[0:NT, c_G, :], in0=SM[0:NT, c_GA, :], in1=NEGA[0:NT, :], op=ALU.mult),
                     reads=[B_sm, B_par], writes=[B_sm])
                P.op("act", I("activation", out=SQ[:, 0:8, 0:NT], in_=Cq[:, :, 0:NT], func=AF.Square),
                     reads=[B_cq], writes=[B_sqz])
                P.op("act", I("activation", out=SQ[:, 8:16, 0:NT], in_=Ckv[:, 0:8, 0:NT], func=AF.Square),
                     reads=[B_ckv], writes=[B_sqz])
                bank, bb = next_back()
                P.op("pe", [I("matmul", bank[0:NT, 2 * j:2 * j + 2], lhsT=SQ[:, j, 0:NT], rhs=ONESB[:, 0:2],
                              start=True, stop=True) for j in range(16)], reads=[B_sqz, B_par], writes=[bb])
                bkv = bank[0:NT, 0:32].rearrange("p (a b) -> p a b", b=2)
                P.op("act", I("activation", out=SM[0:NT, c_RQ:c_RQ + 2, :].rearrange("p a b -> p (a b)"),
                              in_=bkv[:, :, 0], func=AF.Ln, bias=EPSB[0:NT, :], scale=1.0),
                     reads=[bb, B_eps], writes=[B_sm])
                P.op("act", I("activation", out=SM[0:NT, c_RQ:c_RQ + 2, :], in_=SM[0:NT, c_RQ:c_RQ + 2, :],
                              func=AF.Exp, scale=-0.5), reads=[B_sm], writes=[B_sm])
                for zb in range(2):
                    bank, bb = tm_group(3072 + zb * 512, 512, NT, 6 + zb, bank_fn=next_back)
                    P.op("act", I("activation", out=ZS[0:NT, zb * 512:(zb + 1) * 512], in_=bank[0:NT, :], func=AF.Silu),
                         reads=[bb], writes=[B_sqz])
                stage(15)
                pTv = pT[:].rearrange("p (a b) -> p a b", a=8)
                P.op("pe", [I("transpose", pTv[0:NT, h, :], Ckv[:, h, 0:NT], IDB[:, :]) for h in range(8)],
                     reads=[B_ckv, B_idb], writes=[B_pT])
                P.op("dve", I("tensor_tensor", out=KN[0:NT, :, :], in0=pTv[0:NT, :, :],
                              in1=SM[0:NT, c_RK, :].unsqueeze(2).to_broadcast([NT, 8, 128]), op=ALU.mult),
                     reads=[B_pT, B_sm], writes=[B_kn])
                P.op("pe", [I("transpose", pTv[0:NT, h, :], Ckv[:, 8 + h, 0:NT], IDB[:, :]) for h in range(8)],
                     reads=[B_ckv, B_idb], writes=[B_pT])
                P.op("dve", I("tensor_tensor", out=VB[0:NT, :, :], in0=pTv[0:NT, :, :],
                              in1=SM[0:NT, c_BETA, :].unsqueeze(2).to_broadcast([NT, 8, 128]), op=ALU.mult),
                     reads=[B_pT, B_sm], writes=[B_vb])
                P.op("pe", [I("transpose", pTv[:, h, 0:NT], KN[0:NT, h, :], IDB[0:NT, 0:NT]) for h in range(8)],
                     reads=[B_kn, B_idb], writes=[B_pT])
                P.op("act", I("copy", out=KNT[:, :, 0:NT], in_=pTv[:, :, 0:NT]), reads=[B_pT], writes=[B_knt])
                stage(16)
                bank, bb = next_back()
                P.op("pe", I("matmul", bank[0:NT, 0:8], lhsT=MUI[0:NT, 0:NT], rhs=SM[0:NT, c_G, :], start=True, stop=True),
                     reads=[B_const, B_sm], writes=[bb])
                P.op("dve", I("tensor_copy", out=SM[0:NT, c_GC, :], in_=bank[0:NT, 0:8]), reads=[bb], writes=[B_sm])
                P.op("dve", I("tensor_scalar", out=SM[0:NT, c_NGC, :], in0=bank[0:NT, 0:8], scalar1=-1.0, scalar2=None,
                              op0=ALU.mult), reads=[bb], writes=[B_sm])
                GSEL = GO
                P.op("dve", I("tensor_tensor", out=GSEL[0:NT, :, 0:NT],
                              in0=MUI[0:NT, 0:NT].unsqueeze(1).to_broadcast([NT, 8, NT]),
                              in1=SM[0:NT, c_G, :].unsqueeze(2).to_broadcast([NT, 8, NT]), op=ALU.mult),
                     reads=[B_const, B_sm], writes=[B_go])
                P.op("act", I("activation", out=SM[0:NT, c_BE, :], in_=SM[0:NT, c_GC, :], func=AF.Exp),
                     reads=[B_sm], writes=[B_sm])
                P.op("dve", I("tensor_tensor", out=SM[0:NT, c_BE, :], in0=SM[0:NT, c_BE, :], in1=SM[0:NT, c_BETA, :],
                              op=ALU.mult), reads=[B_sm], writes=[B_sm])
                stage(17)
                for hg in range(2):
                    hs = slice(hg * 4, hg * 4 + 4)
                    GRW = pB[:, 0:4 * NT].rearrange("p (h m) -> p h m", h=4)
                    P.op("pe", I("matmul", GRW[:, :, :], lhsT=ONESF[0:NT, :], rhs=GSEL[0:NT, hs, 0:NT],
                                 start=True, stop=True), reads=[B_const, B_go], writes=[B_pB])
                    P.op("act", I("activation", out=EG[:, hs, 0:NT], in_=GRW[:, :, :], func=AF.Exp),
                         reads=[B_pB, B_kn, B_vb], writes=[B_ckv])
                    for c in range(nch):
                        P.op("act", I("copy", out=EGL[:, hs, c], in_=EG[:, hs, 64 * c + 63]),
                             reads=[B_ckv], writes=[B_egl])
                    for c in range(nch):
                        r0 = 64 * c
                        P.op("dve", I("tensor_tensor", out=SM[r0:r0 + 64, c_DL, hs], in0=GRW[r0:r0 + 64, :, r0 + 63],
                                      in1=SM[r0:r0 + 64, c_GC, hs], op=ALU.subtract), reads=[B_pB, B_sm], writes=[B_sm])
                    P.op("act", I("activation", out=SM[0:NT, c_DL, hs], in_=SM[0:NT, c_DL, hs], func=AF.Exp),
                         reads=[B_sm], writes=[B_sm])
                    P.op("dve", I("tensor_tensor", out=D2[0:NT, :, 0:NT], in0=GRW[0:NT, :, :],
                                  in1=SM[0:NT, c_GC, hs].unsqueeze(2).to_broadcast([NT, 4, NT]), op=ALU.subtract),
                         reads=[B_pB, B_sm], writes=[B_d2])
                    P.op("dve", I("tensor_scalar", out=D1[0:NT, :, 0:NT], in0=D2[0:NT, :, 0:NT], scalar1=0.0,
                                  scalar2=-1.0, op0=ALU.max, op1=ALU.mult), reads=[B_d2], writes=[B_d1])
                    P.op("dve", I("tensor_scalar", out=D2[0:NT, :, 0:NT], in0=D2[0:NT, :, 0:NT], scalar1=0.0,
                                  scalar2=None, op0=ALU.min), reads=[B_d2], writes=[B_d2])
                    P.op("act", I("activation", out=D1[0:NT, :, 0:NT], in_=D1[0:NT, :, 0:NT], func=AF.Exp),
                         reads=[B_d1], writes=[B_d1])
                    P.op("act", I("activation", out=D2[0:NT, :, 0:NT], in_=D2[0:NT, :, 0:NT], func=AF.Exp),
                         reads=[B_d2], writes=[B_d2])
                    P.op("dve", I("tensor_tensor", out=D1[0:NT, :, 0:NT], in0=D1[0:NT, :, 0:NT],
                                  in1=MLS[0:NT, 0:NT].unsqueeze(1).to_broadcast([NT, 4, NT]), op=ALU.mult),
                         reads=[B_d1, B_const], writes=[B_d1])
                    P.op("dve", I("tensor_tensor", out=D2[0:NT, :, 0:NT], in0=D2[0:NT, :, 0:NT],
                                  in1=MUI[0:NT, 0:NT].unsqueeze(1).to_broadcast([NT, 4, NT]), op=ALU.mult),
                         reads=[B_d2, B_const], writes=[B_d2])
                    bank, bb = next_back()
                    bkv = bank[:].rearrange("p (a b) -> p a b", a=4)
                    P.op("pe", [I("matmul", bkv[0:NT, hl, 0:NT], lhsT=KNT[:, hg * 4 + hl, 0:NT],
                                  rhs=KNT[:, hg * 4 + hl, 0:NT], start=True, stop=True) for hl in range(4)],
                         reads=[B_knt], writes=[bb])
                    for hl in range(4):
                        P.op("dve", I("scalar_tensor_tensor", out=Xp_w[0:NT, hl, 0:NT], in0=bkv[0:NT, hl, 0:NT],
                                      scalar=SM[0:NT, c_BETA, hg * 4 + hl:hg * 4 + hl + 1], in1=D1[0:NT, hl, 0:NT],
                                      op0=ALU.mult, op1=ALU.mult), reads=[bb, B_d1, B_sm], writes=[B_xp])
                    bank, bb = next_back()
                    bkv = bank[:].rearrange("p (a b) -> p a b", a=4)
                    P.op("pe", [I("matmul", bkv[0:NT, hl, 0:NT], lhsT=KNT[:, hg * 4 + hl, 0:NT],
                                  rhs=Cq[:, hg * 4 + hl, 0:NT], start=True, stop=True) for hl in range(4)],
                         reads=[B_knt, B_cq], writes=[bb])
                    P.op("dve", I("tensor_tensor", out=QKDT[0:NT, :, 0:NT], in0=bkv[0:NT, :, 0:NT],
                                  in1=D2[0:NT, :, 0:NT], op=ALU.mult), reads=[bb, B_d2], writes=[B_qkdt])
                    bank, bb = next_back()
                    bkv = bank[:].rearrange("p (a b) -> p a b", a=4)
                    P.op("pe", [I("transpose", bkv[0:NT, hl, 0:NT], Xp[0:NT, hl, 0:NT], IDF[0:NT, 0:NT])
                                for hl in range(4)], reads=[B_xp, B_const], writes=[bb])
                    P.op("act", I("copy", out=XT_w[0:NT, :, 0:NT], in_=bkv[0:NT, :, 0:NT]), reads=[bb], writes=[B_xt])
                    P.op("dve", I("tensor_tensor", out=PTm_w[0:NT, :, 0:NT],
                                  in0=IDF[0:NT, 0:NT].unsqueeze(1).to_broadcast([NT, 4, NT]),
                                  in1=bkv[0:NT, :, 0:NT], op=ALU.subtract), reads=[bb, B_const], writes=[B_ptm])
                    for lev in range(1, 6):
                        b1, bb1 = next_back()
                        b1v = b1[:].rearrange("p (a b) -> p a b", a=4)
                        P.op("pe", [I("matmul", b1v[:, hl, :], lhsT=XT_w[:, hl, :], rhs=Xp_w[:, hl, :],
                                      start=True, stop=True) for hl in range(4)], reads=[B_xt, B_xp], writes=[bb1])
                        if lev < 5:
                            b2, bb2 = next_back()
                            b2v = b2[:].rearrange("p (a b) -> p a b", a=4)
                            P.op("pe", [I("matmul", b2v[:, hl, :], lhsT=Xp_w[:, hl, :],
                                          rhs=XT_w[:, hl, :], start=True, stop=True) for hl in range(4)],
                                 reads=[B_xt, B_xp], writes=[bb2])
                        P.op("act", I("copy", out=Xp_w[0:NT, :, 0:NT], in_=b1v[0:NT, :, 0:NT]), reads=[bb1], writes=[B_xp])
                        if lev < 5:
                            P.op("dve", I("tensor_copy", out=XT_w[0:NT, :, 0:NT], in_=b2v[0:NT, :, 0:NT]),
                                 reads=[bb2], writes=[B_xt])
                        b3, bb3 = next_back()
                        b3v = b3[:].rearrange("p (a b) -> p a b", a=4)
                        P.op("pe", [I("matmul", b3v[:, hl, :], lhsT=Xp_w[:, hl, :], rhs=PTm_w[:, hl, :],
                                      start=True, stop=True) for hl in range(4)], reads=[B_xp, B_ptm], writes=[bb3])
                        P.op("dve", I("tensor_tensor", out=PTm_w[0:NT, :, 0:NT], in0=PTm[0:NT, :, 0:NT],
                                      in1=b3v[0:NT, :, 0:NT], op=ALU.add), reads=[bb3, B_ptm], writes=[B_ptm])
                    P.op("act", I("copy", out=PTB[0:NT, :, 0:NT], in_=PTm[0:NT, :, 0:NT]), reads=[B_ptm], writes=[B_ptb])
                    P.op("dve", I("tensor_tensor", out=RHSK[0:NT, :, :], in0=KN[0:NT, hs, :],
                                  in1=SM[0:NT, c_BE, hs].unsqueeze(2).to_broadcast([NT, 4, 128]), op=ALU.mult),
                         reads=[B_kn, B_sm], writes=[B_rhsk])
                    P.op("dve", I("tensor_tensor", out=KDEC[0:NT, :, :], in0=KN[0:NT, hs, :],
                                   in1=SM[0:NT, c_DL, hs].unsqueeze(2).to_broadcast([NT, 4, 128]), op=ALU.mult),
                         reads=[B_kn, B_sm], writes=[B_kdec])
                    P.op("dve", I("tensor_tensor", out=QST[:, :, 0:NT], in0=Cq[:, hs, 0:NT], in1=EG[:, hs, 0:NT],
                                  op=ALU.mult), reads=[B_cq, B_ckv], writes=[B_qst])
                    bank, bb = next_back()
                    bkv = bank[:].rearrange("p (a b) -> p a b", a=4)
                    P.op("pe", [I("matmul", bkv[:, hl, 0:NT], lhsT=RHSK[0:NT, hl, :], rhs=PTB[0:NT, hl, 0:NT],
                                  start=True, stop=True) for hl in range(4)], reads=[B_rhsk, B_ptb], writes=[bb])
                    P.op("act", I("copy", out=SKT[:, :, 0:NT], in_=bkv[:, :, 0:NT]), reads=[bb], writes=[B_skt])
                    bank, bb = next_back()
                    bkv = bank[:].rearrange("p (a b) -> p a b", a=4)
                    P.op("pe", [I("matmul", bkv[0:NT, hl, :], lhsT=PTB[0:NT, hl, 0:NT], rhs=VB[0:NT, hg * 4 + hl, :],
                                  start=True, stop=True) for hl in range(4)], reads=[B_vb, B_ptb], writes=[bb])
                    P.op("act", I("copy", out=SOLV[0:NT, :, :], in_=bkv[0:NT, :, :]), reads=[bb], writes=[B_solv])
                    OB = pA[:, 0:512].rearrange("p (a b) -> p a b", a=4)
                    for c in range(nch):
                        r0 = 64 * c
                        bw, bbw = next_back()
                        bwv = bw[:].rearrange("p (a b) -> p a b", a=4)
                        P.op("pe", [I("matmul", bwv[r0:r0 + 64, hl, :], lhsT=SKT[:, hl, r0:r0 + 64],
                                      rhs=SB[:, hg * 4 + hl, :], start=True, stop=True) for hl in range(4)],
                             reads=[B_skt, B_sb], writes=[bbw])
                        P.op("dve", I("tensor_tensor", out=UU[r0:r0 + 64, :, :], in0=SOLV[r0:r0 + 64, :, :],
                                      in1=bwv[r0:r0 + 64, :, :], op=ALU.subtract), reads=[bbw, B_solv], writes=[B_uu])
                        fns = []
                        for hl in range(4):
                            fns.append(I("matmul", OB[r0:r0 + 64, hl, :], lhsT=QST[:, hl, r0:r0 + 64],
                                         rhs=SB[:, hg * 4 + hl, :], start=True, stop=False))
                            fns.append(I("matmul", OB[r0:r0 + 64, hl, :], lhsT=QKDT[0:NT, hl, r0:r0 + 64],
                                         rhs=UU[0:NT, hl, :], start=False, stop=True))
                        P.op("pe", fns, reads=[B_qst, B_sb, B_qkdt, B_uu], writes=[B_pA])
                        bs, bbs = next_back()
                        bsv = bs[:].rearrange("p (a b) -> p a b", a=4)
                        P.op("pe", [I("matmul", bsv[:, hl, :], lhsT=KDEC[r0:r0 + 64, hl, :], rhs=UU[r0:r0 + 64, hl, :],
                                      start=True, stop=True) for hl in range(4)], reads=[B_kdec, B_uu], writes=[bbs])
                        for hl in range(4):
                            h = hg * 4 + hl
                            P.op("dve", I("scalar_tensor_tensor", out=S[:, h, :], in0=S[:, h, :],
                                          scalar=EGL[:, h, c:c + 1], in1=bsv[:, hl, :],
                                          op0=ALU.mult, op1=ALU.add), reads=[bbs, B_s, B_egl], writes=[B_s])
                        P.op("act", I("copy", out=SB[:, hs, :], in_=S[:, hs, :]), reads=[B_s], writes=[B_sb])
                    for hl in range(4):
                        h = hg * 4 + hl
                        P.op("act", I("activation", out=OGZ[0:NT, 0:128], in_=OB[0:NT, hl, :], func=AF.Square,
                                      accum_out=SM[0:NT, c_OSS, h:h + 1]), reads=[B_pA], writes=[B_ogz, B_sm])
                    P.op("dve", I("tensor_tensor", out=SM[0:NT, c_T1, hs], in0=SM[0:NT, c_RQ, hs], in1=SM[0:NT, c_RQ, hs],
                                  op=ALU.mult), reads=[B_sm], writes=[B_sm])
                    P.op("dve", I("tensor_tensor", out=SM[0:NT, c_T1, hs], in0=SM[0:NT, c_T1, hs], in1=SM[0:NT, c_OSS, hs],
                                  op=ALU.mult), reads=[B_sm], writes=[B_sm])
                    P.op("act", I("activation", out=SM[0:NT, c_T1, hs], in_=SM[0:NT, c_T1, hs], func=AF.Ln,
                                  bias=EPSB[0:NT, :], scale=1.0 / (128.0 * 128.0)), reads=[B_sm, B_eps], writes=[B_sm])
                    P.op("act", I("activation", out=SM[0:NT, c_T1, hs], in_=SM[0:NT, c_T1, hs], func=AF.Exp, scale=-0.5),
                         reads=[B_sm], writes=[B_sm])
                    P.op("dve", I("scalar_tensor_tensor", out=SM[0:NT, c_FAC, hs], in0=SM[0:NT, c_RQ, hs],
                                  scalar=128.0 ** -0.5, in1=SM[0:NT, c_T1, hs], op0=ALU.mult, op1=ALU.mult),
                         reads=[B_sm], writes=[B_sm])
                    OG = GO
                    P.op("dve", I("tensor_tensor", out=OG[0:NT, hs, :], in0=OB[0:NT, :, :],
                                  in1=SM[0:NT, c_FAC, hs].unsqueeze(2).to_broadcast([NT, 4, 128]), op=ALU.mult),
                         reads=[B_pA, B_sm], writes=[B_go])
                stage(18)
                P.op("dve", I("tensor_tensor", out=OGZ[0:NT, :], in0=GO[0:NT, :, :].rearrange("p a b -> p (a b)"),
                              in1=ZS[0:NT, :], op=ALU.mult), reads=[B_go, B_sqz], writes=[B_ogz])
                gbank, gbb = next_back()
                gTv = gbank.bitcast(BF16).rearrange("p (a b) -> p a b", a=8)
                P.op("pe", [I("transpose", gTv[:, kc, 0:NT], OGZ[0:NT, kc * 128:(kc + 1) * 128], IDB[0:NT, 0:NT])
                            for kc in range(8)], reads=[B_ogz, B_idb], writes=[gbb])
                P.op("act", I("copy", out=uT[:, :, 0:NT], in_=gTv[:, :, 0:NT]), reads=[gbb], writes=[B_uT])
                stage(19)
                out_proj_post(t, NT, layer, last, uT, B_uT, ZS.bitcast(BF16)[:, 0:D], B_sqz, pB, BF_pB)
                if last:
                    store_x(job, t, NT)
                if t == ntiles - 1 or not prompt:
                    if prompt:
                        dst = bass.AP(o_sc_p.tensor, (i * NPS + job) * 8 * 16384, [[128, 128], [16384, 8], [1, 128]])
                    else:
                        dst = bass.AP(o_sc_s.tensor, (i * NSS + t) * 8 * 16384, [[128, 128], [16384, 8], [1, 128]])
                    P.dma(I("dma_start", out=dst, in_=S[:]), reads=[B_s])
                stage(20)
            P.barrier()

        try:
            stage(1)
            for job in jobs:
                load_x(job)
                P.new_phase()
                for layer in range(n_layers):
                    last = layer == n_layers - 1
                    if layer % 2 == 0:
                        even_layer(job, layer, last)
                    else:
                        odd_layer(job, layer, last)
        except _Stop:
            pass
        P.barrier()
        if os.environ.get("KVERBOSE"):
            print("sem counts", {str(k): v for k, v in P.cnt.items() if v > 2000}, "nsems", len(P.sems), "nops", len(P.recs), "nfill", P.nfill)
        P.replay()
    return nc


def make_consts():
    c = np.zeros((128, 5, 128), np.float32)
    idx = np.arange(128)
    same = (idx[:, None] // 64) == (idx[None, :] // 64)
    c[:, 0, :] = np.eye(128, dtype=np.float32)
    c[:, 1, :] = 1.0
    c[:, 2, :] = (same & (idx[None, :] <= idx[:, None])).astype(np.float32)
    c[:, 3, :] = (same & (idx[None, :] < idx[:, None])).astype(np.float32)
    c[:, 4, :] = (same & (idx[None, :] >= idx[:, None])).astype(np.float32)
    return c


def kernel(x_prompt, x_sample, cache_conv_a, cache_k_b, cache_v_b, state_conv_c, state_s_c,
           norm_pre, norm_post, w_in_even, conv_w_a, rel_bias_b, w_out_even,
           w_in_odd, conv_w_c, a_log_c, dt_bias_c, out_norm_c, w_out_odd, _n_layers=int(os.environ.get("KNL", "4")), _jobs=(0, 1, 2), _cores=NCORES):
    f = lambda a: np.ascontiguousarray(np.asarray(a, dtype=np.float32))
    consts = make_consts()
    shared = dict(norm_pre=f(norm_pre), norm_post=f(norm_post), w_in_even=f(w_in_even), conv_w_a=f(conv_w_a),
                  rel_bias=f(rel_bias_b), w_out_even=f(w_out_even), w_in_odd=f(w_in_odd), conv_w_c=f(conv_w_c),
                  a_log=f(a_log_c), dt_bias=f(dt_bias_c), out_norm=f(out_norm_c), w_out_odd=f(w_out_odd),
                  consts=consts)
    in_maps = []
    for c in range(_cores):
        ps_, ss_ = slice(NPS * c, NPS * (c + 1)), slice(NSS * c, NSS * (c + 1))
        m = dict(shared)
        m["xp"] = f(x_prompt[ps_])
        m["xs"] = f(x_sample[ss_])
        m["cca"] = f(cache_conv_a[:, ss_])
        m["ckb"] = f(cache_k_b[:, ss_]).reshape(2, NSS, 512, 512)
        m["cvb"] = f(cache_v_b[:, ss_]).reshape(2, NSS, 512, 512)
        m["scc"] = f(state_conv_c[:, ss_])
        m["ssc"] = f(state_s_c[:, ss_])
        in_maps.append(m)
    nc = build_program(_n_layers, _jobs)
    res = run_bass_kernel_spmd(nc, in_maps, core_ids=list(range(_cores)))
    R = res.results
    cat0 = lambda k: np.concatenate([r[k] for r in R], axis=0)
    cat1 = lambda k: np.concatenate([r[k] for r in R], axis=1)
    yp = cat0("o_yp")
    ys = cat0("o_ys")
    ca_p = cat1("o_ca_p")
    kb_p = cat1("o_kb_p").reshape(2, -1, 512, 8, 64)
    vb_p = cat1("o_vb_p").reshape(2, -1, 512, 8, 64)
    cc_p = cat1("o_cc_p")
    sc_p = cat1("o_sc_p")
    ca_s = cat1("o_ca_s")
    kb_s = cat1("o_kb_s").reshape(2, -1, DEC, 8, 64)
    vb_s = cat1("o_vb_s").reshape(2, -1, DEC, 8, 64)
    cc_s = cat1("o_cc_s")
    sc_s = cat1("o_sc_s")
    return (yp, ys, ca_p, kb_p, vb_p, cc_p, sc_p, ca_s, kb_s, vb_s, cc_s, sc_s)
```

```python
import contextlib
import os
import numpy as np
import concourse.bass as bass
import concourse.mybir as mybir
from concourse.bass_utils import run_bass_kernel_spmd

F32 = mybir.dt.float32
BF16 = mybir.dt.bfloat16
F32R = mybir.dt.float32r
AF = mybir.ActivationFunctionType
ALU = mybir.AluOpType
AX = mybir.AxisListType

NCORES = 8
D = 1024
SEQ = 2048
NPS = 2
NSS = 4
DEC = 64
EPS = 1e-6
NEG = -30000.0


class Buf:
    __slots__ = ("name", "w", "r", "excl")

    def __init__(self, name, excl=False):
        self.name = name
        self.w = None
        self.r = []
        self.excl = excl


class Prog:
    ENGS = ("pe", "dve", "act", "pool", "sp")
    XLAT = float(os.environ.get('KXLAT', '150'))
    PECOEF = float(os.environ.get('KPECOEF', '0.4'))
    BUCKET = float(os.environ.get('KBUCKET', '1000'))

    def __init__(self, nc, stack, n_dma_sems=12, sched=True):
        self.nc = nc
        self.stack = stack
        self.sched = sched
        self.final = {e: [] for e in self.ENGS}
        self.sems = {}
        self.cnt = {}
        self.waited = {e: {} for e in self.ENGS}
        self.phase = -1
        self.dma_pool = []
        self.dma_rr = 0
        self.n_sw = 0
        self.recs = []
        self.filler = None
        self.nfill = 0
        self.GAPMIN = float(os.environ.get("KGAPMIN", "800"))
        self.FSPACE = float(os.environ.get("KFSPACE", "300"))
        self.seg_start = 0
        self.new_phase()
        for i in range(n_dma_sems):
            k = ("dma", i)
            self._mksem(k)
            self.dma_pool.append(k)

    def _mksem(self, key):
        name = "s_" + "_".join(str(x) for x in key)
        h = self.stack.enter_context(self.nc.semaphore(name))
        self.sems[key] = h
        self.cnt[key] = 0
        return h

    def new_phase(self):
        self.phase += 1
        for e in self.ENGS:
            self._mksem((e, self.phase))

    def _preds(self, eng, reads, writes):
        ps = set()
        for b in reads:
            if b.w is not None:
                ps.add(b.w)
            if b.excl:
                ps.update(d for d in b.r if self.recs[d]["eng"] != eng)
        for b in writes:
            if b.w is not None:
                ps.add(b.w)
            ps.update(b.r)
        return ps

    def _mark(self, oid, reads, writes):
        for b in writes:
            b.w = oid
            b.r = []
        for b in reads:
            if b not in writes:
                b.r.append(oid)

    @staticmethod
    def _free_elems(ap):
        n = 1
        for d in ap.shape[1:]:
            n *= d
        return n

    def _dur(self, eng, fn):
        insts = fn if isinstance(fn, list) else [fn]
        tot = 0.0
        for (name, args, kw) in insts:
            if eng == "pe":
                if name == "transpose":
                    n = self._free_elems(args[1])
                    tot += 64 + self.PECOEF * max(n, 64)
                else:
                    rhs = kw["rhs"]
                    n = self._free_elems(rhs)
                    c = 64 + self.PECOEF * max(n, 64)
                    if rhs.dtype == F32:
                        c *= 4
                    tot += c
            elif eng == "dve":
                ap = kw.get("in0", kw.get("in_", kw.get("out", args[0] if args else None)))
                n = self._free_elems(ap)
                tot += 260 + 0.66 * n * (8 if name == "reciprocal" else 1)
            elif eng == "act":
                ap = kw.get("in_", kw.get("out"))
                n = self._free_elems(ap)
                tot += 330 + 0.52 * n + (100 if kw.get("accum_out") is not None else 0)
            elif eng == "pool":
                ap = kw.get("in0", kw.get("in_", kw.get("out", args[0] if args else None)))
                n = self._free_elems(ap)
                tot += 250 + 2.0 * n
            else:
                tot += 100
        return tot

    def op(self, eng, fn, reads=(), writes=()):
        oid = len(self.recs)
        self.recs.append(dict(eng=eng, fn=fn, preds=self._preds(eng, reads, writes), kind="op",
                              dur=self._dur(eng, fn), lat=0.0, tag=",".join(b.name for b in writes)))
        self._mark(oid, reads, writes)
        return oid

    def dma(self, fn, reads=(), writes=(), queue="sp"):
        oid = len(self.recs)
        out = fn[2]["out"]
        nbytes = 128 * self._free_elems(out) * 4
        self.recs.append(dict(eng=queue, fn=fn, preds=self._preds(queue, reads, writes), kind="dma",
                              dur=(900.0 if queue == "pool" else 120.0), lat=2000.0 + nbytes / 150.0,
                              tag=",".join(b.name for b in writes)))
        self._mark(oid, reads, writes)
        return oid

    def _schedule(self, ids):
        if not self.sched:
            return ids
        recs = self.recs
        idset = set(ids)
        npred = {}
        succs = {i: [] for i in ids}
        for i in ids:
            ps = [p for p in recs[i]["preds"] if p in idset]
            npred[i] = len(ps)
            for p in ps:
                succs[p].append(i)
        import heapq
        rank = {}
        for i in reversed(ids):
            m = 0.0
            for sx in succs[i]:
                if rank[sx] > m:
                    m = rank[sx]
            rank[i] = m + recs[i]["dur"] + recs[i]["lat"]
        ready = {e: [] for e in self.ENGS}
        data_ready = {i: 0.0 for i in ids}
        for i in ids:
            if npred[i] == 0:
                heapq.heappush(ready[recs[i]["eng"]], i)
        efree = {e: 0.0 for e in self.ENGS}
        finish = {}
        order = []
        LOOK = int(os.environ.get('KLOOK', '48'))
        n_left = len(ids)
        while n_left:
            best = None
            for e in self.ENGS:
                h = ready[e]
                if not h:
                    continue
                cands = heapq.nsmallest(LOOK, h)
                for i in cands:
                    st = max(efree[e], data_ready[i])
                    key = (int(st / self.BUCKET), -rank[i], i)
                    if best is None or key < best[0]:
                        best = (key, e, i, st)
            _, e, i, st = best
            ready[e].remove(i)
            heapq.heapify(ready[e])
            r = recs[i]
            efree[e] = st + r["dur"]
            finish[i] = st + r["dur"] + r["lat"]
            r["st"] = st
            r["fin"] = finish[i]
            order.append(i)
            n_left -= 1
            for s in succs[i]:
                lat = 0.0 if recs[s]["eng"] == e and r["kind"] == "op" else self.XLAT
                data_ready[s] = max(data_ready[s], finish[i] + lat)
                npred[s] -= 1
                if npred[s] == 0:
                    heapq.heappush(ready[recs[s]["eng"]], s)
        if os.environ.get("KVERBOSE"):
            busy = {e: 0.0 for e in self.ENGS}
            for i in ids:
                busy[recs[i]["eng"]] += recs[i]["dur"]
            print("segment nops=%d makespan=%.0fus critpath=%.0fus busy(us)=%s" % (
                len(ids), max(finish.values()) / 1e3, max(rank.values()) / 1e3,
                {e: round(v / 1e3) for e, v in busy.items()}), flush=True)
            if os.environ.get("KCRIT") and len(ids) > 3000:
                i = max(ids, key=lambda q: rank[q])
                path = []
                while True:
                    path.append(i)
                    nx = [q for q in succs[i]]
                    if not nx:
                        break
                    i = max(nx, key=lambda q: rank[q])
                from collections import Counter
                cnt = Counter()
                for q in path:
                    f = recs[q]["fn"]
                    f0 = f[0] if isinstance(f, list) else f
                    cnt[(recs[q]["eng"], f0[0])] += recs[q]["dur"] + recs[q]["lat"]
                if os.environ.get("KCRIT") == "2":
                    k0 = len(path) // 2
                    for q in path[k0:k0 + 260]:
                        f = recs[q]["fn"]
                        f0 = f[0] if isinstance(f, list) else f
                        print("   ", q, recs[q]["eng"], f0[0], recs[q]["tag"], round(recs[q]["dur"]))
                print("critpath ops:", len(path), sorted(((round(v / 1e3), k) for k, v in cnt.items()), reverse=True)[:12])
        return order

    def _emit_segment(self):
        ids = list(range(self.seg_start, len(self.recs)))
        self.seg_start = len(self.recs)
        if not ids:
            return
        order = self._schedule(ids)
        recs = self.recs
        for i in order:
            r = recs[i]
            e = r["eng"]
            if r["kind"] == "dma":
                if e == "pool":
                    semkey = ("swdma", self.n_sw)
                    self.n_sw += 1
                    self._mksem(semkey)
                    r["prev"] = 0
                else:
                    semkey = self.dma_pool[self.dma_rr % len(self.dma_pool)]
                    self.dma_rr += 1
                    r["prev"] = self.cnt[semkey]
                self.cnt[semkey] += 16
                r["sem"] = (semkey, self.cnt[semkey])
                r["inc"] = 16
            else:
                k = (e, self.phase)
                self.cnt[k] += 1
                r["sem"] = (k, self.cnt[k])
                r["inc"] = 1
        import bisect
        marks = sorted((recs[i]["fin"], i) for i in order
                       if recs[i]["eng"] in ("dve", "act") and recs[i]["kind"] == "op" and "fin" in recs[i])
        mark_t = [m[0] for m in marks]
        mark_i = [m[1] for m in marks]
        pe_prev_fin = 0.0
        for i in order:
            r = recs[i]
            e = r["eng"]
            need = {}
            for p in r["preds"]:
                pr = recs[p]
                if pr["eng"] == "pe" and e == "pe" and pr["kind"] == "op" and r["kind"] == "op":
                    continue
                k, v = pr["sem"]
                if self.waited[e].get(k, 0) >= v:
                    continue
                if need.get(k, 0) < v:
                    need[k] = v
            if r["kind"] == "dma" and r["prev"] > 0:
                k = r["sem"][0]
                if self.waited[e].get(k, 0) < r["prev"] and need.get(k, 0) < r["prev"]:
                    need[k] = r["prev"]
            if e == "pe" and self.filler is not None and self.sched and "st" in r:
                gap = r["st"] - pe_prev_fin
                if gap > self.GAPMIN:
                    nmark = min(int(gap / self.FSPACE), 24)
                    for m in range(1, nmark + 1):
                        tm = pe_prev_fin + m * (gap / (nmark + 1))
                        j = bisect.bisect_right(mark_t, tm) - 1
                        if j < 0:
                            continue
                        q = recs[mark_i[j]]
                        k, v = q["sem"]
                        fw = []
                        if self.waited[e].get(k, 0) < v:
                            fw.append((k, v))
                            self.waited[e][k] = v
                        self.final[e].append((fw, self.filler, None, 0))
                        self.nfill += 1
                pe_prev_fin = r["st"] + r["dur"]
            need = {k: v for k, v in need.items() if self.waited[e].get(k, 0) < v}
            for k, v in need.items():
                self.waited[e][k] = v
            self.final[e].append((list(need.items()), r["fn"], r["sem"][0], r["inc"]))

    def barrier(self):
        self._emit_segment()
        allk = [(k, v) for k, v in self.cnt.items() if v > 0]
        for e in self.ENGS:
            waits = []
            for k, v in allk:
                if self.waited[e].get(k, 0) < v:
                    waits.append((k, v))
                    self.waited[e][k] = v
            self.final[e].append((waits, None, None, 0))

    def replay(self):
        nc = self.nc
        engmap = {"pe": nc.tensor, "dve": nc.vector, "act": nc.scalar, "pool": nc.gpsimd, "sp": nc.sync}

        def run(ename):
            eng = engmap[ename]
            for (waits, fn, k, inc) in self.final[ename]:
                for (wk, wv) in waits:
                    eng.wait_ge(self.sems[wk], wv)
                if fn is None:
                    continue
                insts = fn if isinstance(fn, list) else [fn]
                inst = None
                for (name, args, kw) in insts:
                    inst = getattr(eng, name)(*args, **kw)
                if inc:
                    inst.then_inc(self.sems[k], inc)

        with nc.Block() as block:
            @block.tensor
            def _(e):
                run("pe")

            @block.vector
            def _(e):
                run("dve")

            @block.scalar
            def _(e):
                run("act")

            @block.gpsimd
            def _(e):
                run("pool")

            @block.sync
            def _(e):
                run("sp")


import os
_STOP = float(os.environ.get("KSTOP", "0"))


class _Stop(Exception):
    pass


def stage(n):
    if _STOP and n >= _STOP:
        raise _Stop()


def I(name, *args, **kw):
    return (name, args, kw)


ARENA_WORDS_CFG = 12576


def build_program(n_layers=4, jobs=(0, 1, 2)):
    nc = bass.Bass("TRN2", target_bir_lowering=False)

    def din(name, shape):
        return nc.dram_tensor(name, list(shape), F32, kind="ExternalInput").ap()

    def dout(name, shape):
        return nc.dram_tensor(name, list(shape), F32, kind="ExternalOutput").ap()

    xp = din("xp", [NPS, SEQ, D])
    xs = din("xs", [NSS, DEC, D])
    cca = din("cca", [2, NSS, 2, 512])
    ckb = din("ckb", [2, NSS, 512, 512])
    cvb = din("cvb", [2, NSS, 512, 512])
    scc = din("scc", [2, NSS, 3, 3072])
    ssc = din("ssc", [2, NSS, 8, 128, 128])
    norm_pre = din("norm_pre", [4, D])
    norm_post = din("norm_post", [4, D])
    w_in_even = din("w_in_even", [2, D, 4096])
    conv_w_a = din("conv_w_a", [2, 3, 512])
    rel_bias = din("rel_bias", [2, 8, 513])
    w_out_even = din("w_out_even", [2, D, D])
    w_in_odd = din("w_in_odd", [2, D, 4112])
    conv_w_c = din("conv_w_c", [2, 4, 3072])
    a_log = din("a_log", [2, 8])
    dt_bias = din("dt_bias", [2, 8])
    out_norm = din("out_norm", [2, 128])
    w_out_odd = din("w_out_odd", [2, D, D])
    consts = din("consts", [128, 5, 128])

    o_yp = dout("o_yp", [NPS, SEQ, D])
    o_ys = dout("o_ys", [NSS, DEC, D])
    o_ca_p = dout("o_ca_p", [2, NPS, 2, 512])
    o_kb_p = dout("o_kb_p", [2, NPS, 512, 512])
    o_vb_p = dout("o_vb_p", [2, NPS, 512, 512])
    o_cc_p = dout("o_cc_p", [2, NPS, 3, 3072])
    o_sc_p = dout("o_sc_p", [2, NPS, 8, 128, 128])
    o_ca_s = dout("o_ca_s", [2, NSS, 2, 512])
    o_kb_s = dout("o_kb_s", [2, NSS, DEC, 512])
    o_vb_s = dout("o_vb_s", [2, NSS, DEC, 512])
    o_cc_s = dout("o_cc_s", [2, NSS, 3, 3072])
    o_sc_s = dout("o_sc_s", [2, NSS, 8, 128, 128])

    btd = nc.dram_tensor("btd", [2, 128, 8 * 5 * 128], BF16, kind="Internal").ap()
    extd = nc.dram_tensor("extd", [16, 128, 768], F32, kind="Internal").ap()

    with contextlib.ExitStack() as st:
        P = Prog(nc, st, sched=os.environ.get("KSCHED", "1") == "1")

        def sb(stack, name, shape, dt=F32):
            return stack.enter_context(nc.sbuf_tensor(name, list(shape), dt))

        def ps(name, shape, dt=F32):
            return st.enter_context(nc.psum_tensor(name, list(shape), dt))

        X = sb(st, "X", [128, 16, D])
        WIN = sb(st, "WIN", [128, 8, 4112], BF16)
        WOUT = sb(st, "WOUT", [128, 8, D], BF16)
        NEU = sb(st, "NEU", [128, 3, 512], F32R)
        IDB = sb(st, "IDB", [128, 128], BF16)
        GT = sb(st, "GT", [128, 4, 8])
        EPSB = sb(st, "EPSB", [128, 1])
        hn = sb(st, "hn", [128, D], BF16)
        hT = sb(st, "hT", [128, 8, 128], BF16)
        uT = sb(st, "uT", [128, 8, 128], BF16)
        GP = sb(st, "GP", [128, D], BF16)
        SMALL = sb(st, "SMALL", [128, 16])

        B_x = [Buf("x%d" % t) for t in range(16)]
        B_win = [Buf("win%d" % g) for g in range(9)]
        B_wout = Buf("wout")
        B_const = Buf("const")
        B_idb = Buf("idb")
        B_gt = Buf("gt")
        B_eps = Buf("eps")
        B_hn = Buf("hn")
        B_hT = Buf("hT")
        B_uT = Buf("uT")
        B_gp = Buf("gp")
        B_tmp = Buf("tmp")
        B_junk = Buf("junk")
        B_small = Buf("small")
        B_small_pre = B_small

        ARENA_WORDS = [0]
        used = [0]

        class _Arena:
            def __init__(self):
                self.off = 0
                self.t = None

            def reset(self):
                self.off = 0

            def alloc(self, stack_unused, name, shape, dt=F32):
                n = 1
                for d in shape[1:]:
                    n *= d
                words = n if dt == F32 else (n + 1) // 2
                ap = self.t[:, self.off:self.off + words]
                self.off += words
                assert self.off <= ARENA_N, ("arena overflow", name, self.off)
                if dt != F32:
                    ap = ap.bitcast(dt)
                if len(shape) > 2:
                    names = " ".join("d%d" % k for k in range(len(shape) - 1))
                    kw = {"d%d" % k: shape[k + 1] for k in range(len(shape) - 2)}
                    ap = ap.rearrange("p (%s) -> p %s" % (names, names), **kw)
                return ap

        ARENA_N = ARENA_WORDS_CFG
        arena = _Arena()
        arena.t = sb(st, "ARENA", [128, ARENA_N])

        pT = ps("pT", [128, 1024], BF16)
        pg = [ps("pg%d" % i, [128, 512]) for i in range(3)]
        pA = ps("pA", [128, 1024])
        pB = ps("pB", [128, 1024])
        B_pT = Buf("pT", True)
        B_pg = [Buf("pg%d" % i, True) for i in range(3)]
        B_pA = Buf("pA", True)
        B_pB = Buf("pB", True)
        B_pA1 = Buf("pA1", True)
        B_pB1 = Buf("pB1", True)
        BF_pA = [B_pA, B_pA1]
        BF_pB = [B_pB, B_pB1]
        pg_rr = [0]

        def next_pg():
            i = pg_rr[0] % 2
            pg_rr[0] += 1
            return pg[i], B_pg[i]

        fr_rr = [0]
        bk_rr = [0]

        def next_front():
            i = fr_rr[0] % 2
            fr_rr[0] += 1
            return pg[i], B_pg[i]

        def next_back():
            i = bk_rr[0] % 2
            bk_rr[0] += 1
            return [(pA[:, 512:1024], B_pA1), (pB[:, 512:1024], B_pB1)][i]

        arena.reset()
        CONST0 = arena.alloc(None, "CONST0", [128, 5, 128])
        P.dma(I("dma_start", out=CONST0[:], in_=consts), writes=[B_const])
        P.op("dve", I("tensor_copy", out=IDB[:], in_=CONST0[:, 0, :]), reads=[B_const], writes=[B_idb])
        P.barrier()
        P.op("pool", I("memset", EPSB[:], EPS), writes=[B_eps])
        for l4 in range(4):
            gt_src = bass.AP(norm_pre.tensor, l4 * D, [[1, 128], [128, 8]])
            P.dma(I("dma_start", out=GT[:, l4, :], in_=gt_src, allow_slow_non_contiguous=True), writes=[B_gt])

        if os.environ.get("KFILL", "1") == "1":
            nf = int(os.environ.get("KNF", "4"))
            P.filler = [I("matmul", pg[2][:, 0:128], lhsT=IDB[:, :], rhs=IDB[:, :], start=True, stop=True)
                        for _ in range(nf)]
        if n_layers >= 1:
            if True:
                arena.reset()
                ts = None
                EXT = [arena.alloc(ts, "EXT%d" % i, [128, 768]) for i in range(2)]
                BTF = [arena.alloc(ts, "BTF%d" % i, [128, 5, 128]) for i in range(2)]
                BTB = arena.alloc(ts, "BTB", [128, 8, 5, 128], BF16)
                B_ext = [Buf("ext0"), Buf("ext1")]
                B_btf = [Buf("btf0"), Buf("btf1")]
                B_btb = Buf("btb")
                B_extd = [Buf("extd%d" % i) for i in range(16)]
                for li in range(2):
                    for h in range(8):
                        idx = li * 8 + h
                        e = idx % 2
                        ext, bext, btf, bbtf = EXT[e], B_ext[e], BTF[e], B_btf[e]
                        P.dma(I("dma_start",
                            out=ext[:, 0:384], in_=rel_bias[li, h:h + 1, 129:513].to_broadcast([128, 384])),
                            writes=[bext])
                        P.op("dve", I("tensor_copy",
                            out=ext[:, 384:768], in_=ext[:, 383:384].to_broadcast([128, 384])),
                            reads=[bext], writes=[bext])
                        P.dma(I("dma_start", out=extd[idx], in_=ext[:]),
                              reads=[bext], writes=[B_extd[idx]])
                        src = bass.AP(extd.tensor, idx * 128 * 768 + 127, [[767, 128], [128, 5], [1, 128]])
                        P.dma(I("dma_start", out=btf[:], in_=src),
                              reads=[B_extd[idx]], writes=[bbtf])
                        P.op("pool", I("memset", btf[64:128, 0, 0:64], NEG),
                             writes=[bbtf])
                        P.op("pool", I("memset", btf[0:64, 4, 64:128], NEG),
                             writes=[bbtf])
                        P.op("act", I("copy", out=BTB[:, h, :, :], in_=btf[:]),
                             reads=[bbtf], writes=[B_btb])
                    P.dma(I("dma_start",
                        out=btd[li], in_=BTB[:].rearrange("p a b c -> p (a b c)")), reads=[B_btb], writes=[Buf("btd")])
                P.barrier()

        def load_x(job):
            if job < 2:
                for t in range(16):
                    P.dma(I("dma_start", out=X[:, t, :], in_=xp[job, t * 128:(t + 1) * 128, :]),
                          writes=[B_x[t]])
            else:
                for t in range(NSS):
                    P.dma(I("dma_start", out=X[0:DEC, t, :], in_=xs[t]), writes=[B_x[t]])

        def store_x(job, t, NT):
            if job < 2:
                P.dma(I("dma_start", out=o_yp[job, t * 128:(t + 1) * 128, :], in_=X[0:NT, t, :]),
                      reads=[B_x[t]])
            else:
                P.dma(I("dma_start", out=o_ys[t], in_=X[0:NT, t, :]), reads=[B_x[t]])

        def load_weights(w_in, ncols, w_out, layer):
            wv = w_in.rearrange("(kc p) n -> p kc n", p=128)
            c0 = 0
            while c0 < ncols:
                c1 = min(ncols, c0 + 2048)
                gs = list(range(c0 // 512, (c1 + 511) // 512))
                P.dma(I("dma_start", out=WIN[:, :, c0:c1], in_=wv[:, :, c0:c1], max_dma_last_dim=8192),
                      writes=[B_win[g] for g in gs], queue="pool")
                c0 = c1
            wo = w_out.rearrange("(kc p) n -> p kc n", p=128)
            P.dma(I("dma_start", out=WOUT[:], in_=wo, max_dma_last_dim=8192), writes=[B_wout], queue="pool")

        def load_gp(layer, tmp, btmp):
            P.dma(I("dma_start", out=tmp, in_=norm_post[layer:layer + 1, :].to_broadcast([128, D])), writes=btmp)
            P.op("act", I("copy", out=GP[:], in_=tmp), reads=btmp, writes=[B_gp])

        B_small2 = Buf("small_post")

        def rstd(NT, src_ap, src_bufs, col, scale, junk=None, bjunk=None):
            B_small = B_small_pre if col == 0 else B_small2
            if junk is None:
                junk, bjunk = hn, B_hn
            P.op("act", I("activation", out=junk[0:NT, :], in_=src_ap, func=AF.Square,
                          accum_out=SMALL[0:NT, col:col + 1]), reads=src_bufs, writes=[bjunk, B_small])
            P.op("act", I("activation", out=SMALL[0:NT, col + 1:col + 2], in_=SMALL[0:NT, col:col + 1],
                          func=AF.Ln, bias=EPSB[0:NT, :], scale=scale), reads=[B_small, B_eps], writes=[B_small])
            P.op("act", I("activation", out=SMALL[0:NT, col + 1:col + 2], in_=SMALL[0:NT, col + 1:col + 2],
                          func=AF.Exp, scale=-0.5), reads=[B_small], writes=[B_small])

        def pre_norm(t, NT, layer):
            rstd(NT, X[0:NT, t, :], [B_x[t]], 0, 1.0 / D)
            P.op("dve", I("tensor_scalar", out=hn[0:NT, :], in0=X[0:NT, t, :],
                                                        scalar1=SMALL[0:NT, 1:2], scalar2=None, op0=ALU.mult),
                 reads=[B_x[t], B_small], writes=[B_hn])
            pTv = pT[:].rearrange("p (a b) -> p a b", a=8)
            P.op("pe", [I("transpose", pTv[:, kc, 0:NT], hn[0:NT, kc * 128:(kc + 1) * 128],
                                                                  IDB[0:NT, 0:NT]) for kc in range(8)],
                 reads=[B_hn, B_idb], writes=[B_pT])
            P.op("dve", I("tensor_tensor",
                out=hT[:, :, 0:NT], in0=pTv[:, :, 0:NT],
                in1=GT[:, layer, :].unsqueeze(2).to_broadcast([128, 8, NT]), op=ALU.mult),
                reads=[B_pT, B_gt], writes=[B_hT])

        def fm_group(c0, NT, g, nft=4, bank_fn=None):
            bank, bbank = (bank_fn or next_pg)()
            bv = bank[:].rearrange("p (a b) -> p a b", a=4)
            fns = []
            for ft in range(nft):
                for kc in range(8):
                    fns.append(I("matmul",
                        bv[:, ft, 0:NT], lhsT=WIN[:, kc, c0 + ft * 128:c0 + (ft + 1) * 128], rhs=hT[:, kc, 0:NT],
                        start=(kc == 0), stop=(kc == 7)))
            P.op("pe", fns, reads=[B_hT, B_win[g]], writes=[bbank])
            return bv, bbank

        def tm_group(c0, ncol, NT, g, bank_fn=None):
            bank, bbank = (bank_fn or next_pg)()
            fns = []
            for kc in range(8):
                fns.append(I("matmul",
                    bank[0:NT, 0:ncol], lhsT=hT[:, kc, 0:NT], rhs=WIN[:, kc, c0:c0 + ncol],
                    start=(kc == 0), stop=(kc == 7)))
            P.op("pe", fns, reads=[B_hT, B_win[g]], writes=[bbank])
            return bank, bbank

        def out_proj_post(t, NT, layer, last, uT=uT, B_uT=B_uT, junk=None, bjunk=None, Y=None, BY=None):
            if Y is None:
                Y, BY = pA, BF_pA
            fns = []
            for nb in range(2):
                for kc in range(8):
                    fns.append(I("matmul",
                        Y[0:NT, nb * 512:(nb + 1) * 512], lhsT=uT[:, kc, 0:NT], rhs=WOUT[:, kc, nb * 512:(nb + 1) * 512],
                        start=(kc == 0), stop=(kc == 7)))
            P.op("pe", fns, reads=[B_uT, B_wout], writes=BY)
            rstd(NT, Y[0:NT, :], BY, 2, 1.0 / D, junk, bjunk)
            P.op("dve", I("scalar_tensor_tensor",
                out=Y[0:NT, :], in0=Y[0:NT, :], scalar=SMALL[0:NT, 3:4], in1=GP[0:NT, :],
                op0=ALU.mult, op1=ALU.mult), reads=BY + [B_small2, B_gp], writes=BY)
            P.op("dve", I("tensor_tensor", out=X[0:NT, t, :], in0=X[0:NT, t, :], in1=Y[0:NT, :],
                          op=ALU.add), reads=[B_x[t]] + BY, writes=[B_x[t]])

        def interleave(front_gens, back_gens):
            n = len(back_gens)
            for _ in front_gens[0]:
                pass
            for t in range(n):
                b = back_gens[t]
                f = front_gens[t + 1] if t + 1 < n else iter(())
                bdone = fdone = False
                while not (bdone and fdone):
                    if not bdone:
                        try:
                            next(b)
                        except StopIteration:
                            bdone = True
                    if not fdone:
                        try:
                            next(f)
                        except StopIteration:
                            fdone = True

        def even_layer(job, layer, last):
            i = layer // 2
            prompt = job < 2
            NT = 128 if prompt else DEC
            ntiles = 16 if prompt else NSS
            load_weights(w_in_even[i], 4096, w_out_even[i], layer)
            arena.reset()
            ls = None
            sb = arena.alloc
            KT = sb(ls, "KT", [128, 4, 6 * 128], BF16)
            V = sb(ls, "V", [128, 6, 8, 65], BF16)
            BT = sb(ls, "BT", [128, 8, 5, 128], BF16)
            CW = sb(ls, "CW", [128, 4, 3])
            U = sb(ls, "U", [128, 4, 130])
            TC = sb(ls, "TC", [128, 8, 128])
            TAC, CV = TC[:, 0:4, :], TC[:, 4:8, :]
            _tmpgp = TC.rearrange("p a b -> p (a b)")
            SZ = sb(ls, "SZ", [128, 4, 128])
            E = [sb(ls, "E%d" % k, [128, 5, 128]) for k in range(2)]
            PTb = [sb(ls, "PTb%d" % k, [128, 5, 128], BF16) for k in range(2)]
            ON = sb(ls, "ON", [128, 8, 64], BF16)
            REC = sb(ls, "REC", [128, 8])
            nbuf = 2 if prompt else 1
            QTs = [sb(ls, "QT%d" % k, [128, 4, 128], BF16) for k in range(nbuf)]
            SZBs = [sb(ls, "SZB%d" % k, [128, 4, 128], BF16) for k in range(nbuf)]
            uTs = [uT] + [sb(ls, "uT%d" % k, [128, 8, 128], BF16) for k in range(1, nbuf)]
            B_uTs = [B_uT] + [Buf("uT%d" % k) for k in range(1, nbuf)]
            if prompt:
                JK = NEU[:, 0:2, :].rearrange("p a b -> p (a b)")
                B_jk = Buf("jk")
            else:
                JK, B_jk = hn, B_hn
                KC = sb(ls, "KC", [128, 4, 512], BF16)
                VC = TC.rearrange("p a b -> p (a b)").bitcast(BF16).rearrange("p (j c) -> p j c", j=4)
            B_kt = [Buf("kt%d" % k) for k in range(6)]
            B_v = [Buf("v%d" % k) for k in range(6)]
            B_bt, B_cw, B_u, B_tac, B_cv, B_sz = (Buf(n) for n in ("bt", "cw", "u", "tac", "cv", "sz"))
            load_gp(layer, _tmpgp, [B_tac, B_cv])
            B_qts = [Buf("qt%d" % k) for k in range(nbuf)]
            B_szbs = [Buf("szb%d" % k) for k in range(nbuf)]
            B_e = [Buf("e0"), Buf("e1")]
            B_ptb = [Buf("ptb0"), Buf("ptb1")]
            B_on, B_rec, B_kc = Buf("on"), Buf("rec"), Buf("kc")
            B_kvo = B_e
            KVO = [E[k][:, 0:4, :].rearrange("p a b -> p (a b)") for k in range(2)]
            pTv = pT[:].rearrange("p (a b) -> p a b", a=8)

            P.dma(I("dma_start", out=BT[:].rearrange("p a b c -> p (a b c)"), in_=btd[i]), writes=[B_bt])
            for j in range(3):
                cw_src = bass.AP(conv_w_a.tensor, (i * 3 + j) * 512, [[1, 128], [128, 4]])
                P.dma(I("dma_start", out=CW[:, :, j], in_=cw_src, allow_slow_non_contiguous=True), writes=[B_cw])
            P.op("pool", I("memset", V[:], 1.0), writes=B_v)
            P.op("pool", I("memset", KT[:], 0.0), writes=B_kt)
            P.op("pool", I("memset", U[:], 0.0), writes=[B_u])

            def tile_cfg(t):
                if prompt:
                    ns = min(5, t + 1)
                    return t % 6, ns, [(t - s) % 6 for s in range(ns)]
                return 4, 5, [4 - s for s in range(5)]

            def front(t):
                out_tile = (not prompt) or t >= 12
                kslot_new, ns, kslots = tile_cfg(t)
                QT, B_qt = QTs[t % nbuf], B_qts[t % nbuf]
                SZB, B_szb = SZBs[t % nbuf], B_szbs[t % nbuf]
                uTt, B_uTt = uTs[t % nbuf], B_uTs[t % nbuf]
                if not prompt:
                    for j in range(2):
                        hsrc = bass.AP(cca.tensor, ((i * NSS + t) * 2 + j) * 512, [[1, 128], [128, 4]])
                        P.dma(I("dma_start", out=U[:, :, j], in_=hsrc, allow_slow_non_contiguous=True), writes=[B_u])
                    kcv = ckb[i, t].rearrange("(j p) c -> p j c", p=128)
                    P.dma(I("dma_start", out=KC[:], in_=kcv, max_dma_last_dim=2048), writes=[B_kc], queue="pool")
                    vcv = cvb[i, t].rearrange("(j p) c -> p j c", p=128)
                    P.dma(I("dma_start", out=VC[:], in_=vcv, max_dma_last_dim=2048), writes=[B_tac, B_cv], queue="pool")
                    P.op("dve", I("tensor_copy", out=V[:, 0:4, :, 0:64],
                                  in_=VC[:].rearrange("p j (h d) -> p j h d", h=8)), reads=[B_tac, B_cv], writes=B_v[0:4])
                    for r in range(2):
                        fns = []
                        for jj in range(2):
                            for hp in range(4):
                                j = 2 * r + jj
                                fns.append(I("transpose", pTv[:, jj * 4 + hp, :], KC[:, j, hp * 128:(hp + 1) * 128],
                                             IDB[:]))
                        P.op("pe", fns, reads=[B_kc, B_idb], writes=[B_pT])
                        for jj in range(2):
                            j = 2 * r + jj
                            P.op("act", I("copy", out=KT[:, :, j * 128:(j + 1) * 128], in_=pTv[:, jj * 4:jj * 4 + 4, :]),
                                 reads=[B_pT], writes=[B_kt[j]])
                pre_norm(t, NT, layer)
                yield
                bv, bb = fm_group(1536, NT, 3)
                P.op("act", I("activation", out=SZ[:, :, 0:NT], in_=bv[:, :, 0:NT], func=AF.Silu),
                     reads=[bb], writes=[B_sz])
                bv, bb = fm_group(3584, NT, 7)
                P.op("act", I("activation", out=SZB[:, :, 0:NT], in_=bv[:, :, 0:NT], func=AF.Silu),
                     reads=[bb], writes=[B_szb])
                yield
                bv, bb = fm_group(512, NT, 1)
                P.op("act", I("copy", out=TAC[:, :, 0:NT], in_=bv[:, :, 0:NT]), reads=[bb], writes=[B_tac])
                yield
                bv, bb = fm_group(1024, NT, 2)
                P.op("dve", I("tensor_tensor", out=U[:, :, 2:2 + NT], in0=bv[:, :, 0:NT], in1=TAC[:, :, 0:NT],
                              op=ALU.mult), reads=[bb, B_tac], writes=[B_u])
                for ct in range(4):
                    P.op("dve", I("tensor_scalar", out=CV[:, ct, 0:NT], in0=U[:, ct, 0:NT], scalar1=CW[:, ct, 0:1],
                                  scalar2=None, op0=ALU.mult), reads=[B_u, B_cw], writes=[B_cv])
                    for j in (1, 2):
                        P.op("dve", I("scalar_tensor_tensor", out=CV[:, ct, 0:NT], in0=U[:, ct, j:j + NT],
                                      scalar=CW[:, ct, j:j + 1], in1=CV[:, ct, 0:NT], op0=ALU.mult, op1=ALU.add),
                             reads=[B_u, B_cw, B_cv], writes=[B_cv])
                if t == ntiles - 1 or not prompt:
                    for j in range(2):
                        if prompt:
                            dst = bass.AP(o_ca_p.tensor, ((i * NPS + job) * 2 + j) * 512, [[1, 128], [128, 4]])
                        else:
                            dst = bass.AP(o_ca_s.tensor, ((i * NSS + t) * 2 + j) * 512, [[1, 128], [128, 4]])
                        P.dma(I("dma_start", out=dst, in_=U[:, :, NT + j], allow_slow_non_contiguous=True), reads=[B_u])
                if prompt and t < ntiles - 1:
                    P.op("pool", I("tensor_copy", out=U[:, :, 0:2], in_=U[:, :, NT:NT + 2]), reads=[B_u], writes=[B_u])
                yield
                bv, bb = fm_group(0, NT, 0)
                P.op("dve", I("tensor_tensor", out=CV[:, :, 0:NT], in0=bv[:, :, 0:NT], in1=CV[:, :, 0:NT], op=ALU.mult),
                     reads=[bb, B_cv], writes=[B_cv])
                P.op("pool", I("tensor_tensor", out=uTt[:, 0:4, 0:NT], in0=CV[:, :, 0:NT], in1=SZ[:, :, 0:NT],
                               op=ALU.mult), reads=[B_cv, B_sz], writes=[B_uTt])
                yield
                bv, bb = fm_group(2048, NT, 4)
                P.op("act", I("activation", out=QT[:, :, 0:NT], in_=bv[:, :, 0:NT], func=AF.Copy, scale=0.125),
                     reads=[bb], writes=[B_qt])
                yield
                bv, bb = fm_group(2560, NT, 5)
                P.op("dve", I("tensor_copy", out=KT[:, :, kslot_new * 128:kslot_new * 128 + NT], in_=bv[:, :, 0:NT]),
                     reads=[bb], writes=[B_kt[kslot_new]])
                yield
                bank, bb = tm_group(3072, 512, NT, 6)
                P.op("dve", I("tensor_copy", out=V[0:NT, kslot_new, :, 0:64],
                              in_=bank[0:NT, :].rearrange("p (h d) -> p h d", h=8)), reads=[bb], writes=[B_v[kslot_new]])
                if out_tile:
                    P.op("act", I("copy", out=KVO[0][0:NT, :], in_=bank[0:NT, :]), reads=[bb], writes=[B_kvo[0]])
                    dst = o_vb_p[i, job, (t - 12) * 128:(t - 11) * 128, :] if prompt else o_vb_s[i, t]
                    P.dma(I("dma_start", out=dst, in_=KVO[0][0:NT, :]), reads=[B_kvo[0]])
                    yield
                    bank, bb = tm_group(2560, 512, NT, 5)
                    P.op("act", I("copy", out=KVO[1][0:NT, :], in_=bank[0:NT, :]), reads=[bb], writes=[B_kvo[1]])
                    dst = o_kb_p[i, job, (t - 12) * 128:(t - 11) * 128, :] if prompt else o_kb_s[i, t]
                    P.dma(I("dma_start", out=dst, in_=KVO[1][0:NT, :]), reads=[B_kvo[1]])
                yield

            def back(t):
                kslot_new, ns, kslots = tile_cfg(t)
                QT, B_qt = QTs[t % nbuf], B_qts[t % nbuf]
                SZB, B_szb = SZBs[t % nbuf], B_szbs[t % nbuf]
                uTt, B_uTt = uTs[t % nbuf], B_uTs[t % nbuf]
                stv = pA[:].rearrange("p (a b) -> p a b", a=8)
                povs = [pB[:, 0:260].rearrange("p (a b) -> p a b", a=4),
                        pB[:, 512:772].rearrange("p (a b) -> p a b", a=4)]

                def emit_pv(h):
                    hl = h % 4
                    pov = povs[h // 4]
                    fns = []
                    for s in range(ns):
                        fns.append(I("matmul", pov[0:NT, hl, :], lhsT=PTb[h % 2][:, s, 0:NT], rhs=V[:, kslots[s], h, :],
                                     start=(s == 0), stop=(s == ns - 1)))
                    P.op("pe", fns, reads=[B_ptb[h % 2]] + [B_v[k] for k in kslots], writes=BF_pB)

                for h in range(8):
                    hp, po = h // 2, (h % 2) * 64
                    fns = []
                    for s in range(ns):
                        ks = kslots[s]
                        fns.append(I("matmul", stv[:, s, 0:NT], lhsT=KT[po:po + 64, hp, ks * 128:(ks + 1) * 128],
                                     rhs=QT[po:po + 64, hp, 0:NT], start=True, stop=True))
                    P.op("pe", fns, reads=[B_qt] + [B_kt[k] for k in kslots], writes=BF_pA)
                    P.op("dve", I("tensor_tensor", out=E[h % 2][:, 0:ns, 0:NT], in0=stv[:, 0:ns, 0:NT],
                                  in1=BT[:, h, 0:ns, 0:NT], op=ALU.add), reads=BF_pA + [B_bt], writes=[B_e[h % 2]])
                    P.op("act", I("activation", out=PTb[h % 2][:, 0:ns, 0:NT], in_=E[h % 2][:, 0:ns, 0:NT], func=AF.Exp),
                         reads=[B_e[h % 2]], writes=[B_ptb[h % 2]])
                    if h > 0:
                        emit_pv(h - 1)
                    yield
                emit_pv(7)
                for half in range(2):
                    pov = povs[half]
                    P.op("dve", I("reciprocal", out=REC[0:NT, half * 4:half * 4 + 4], in_=pov[0:NT, :, 64]),
                         reads=BF_pB, writes=[B_rec])
                    P.op("dve", I("tensor_tensor", out=ON[0:NT, half * 4:half * 4 + 4, :], in0=pov[0:NT, :, 0:64],
                                  in1=REC[0:NT, half * 4:half * 4 + 4].unsqueeze(2).to_broadcast([NT, 4, 64]),
                                  op=ALU.mult), reads=BF_pB + [B_rec], writes=[B_on])
                onf = ON[:].rearrange("p a b -> p (a b)")
                P.op("pe", [I("transpose", pTv[:, ft, 0:NT], onf[0:NT, ft * 128:(ft + 1) * 128], IDB[0:NT, 0:NT])
                            for ft in range(4)], reads=[B_on, B_idb], writes=[B_pT])
                P.op("dve", I("tensor_tensor", out=uTt[:, 4:8, 0:NT], in0=pTv[:, 0:4, 0:NT], in1=SZB[:, :, 0:NT],
                              op=ALU.mult), reads=[B_pT, B_szb], writes=[B_uTt])
                yield
                out_proj_post(t, NT, layer, last, uTt, B_uTt, JK, B_jk)
                if last:
                    store_x(job, t, NT)
                yield

            if prompt:
                interleave([front(t) for t in range(ntiles)], [back(t) for t in range(ntiles)])
            else:
                for t in range(ntiles):
                    for _ in front(t):
                        pass
                    for _ in back(t):
                        pass
            P.barrier()

        def odd_layer(job, layer, last):
            i = layer // 2
            prompt = job < 2
            NT = 128 if prompt else DEC
            nch = NT // 64
            ntiles = 16 if prompt else NSS
            load_weights(w_in_odd[i], 4112, w_out_odd[i], layer)
            arena.reset()
            al = arena.alloc
            CONST = al(None, "CONST", [128, 5, 128])
            P.dma(I("dma_start", out=CONST[:], in_=consts), writes=[B_const])
            S = al(None, "S", [128, 8, 128])
            SB = al(None, "SB", [128, 8, 128], BF16)
            RH = al(None, "RH", [128, 24, 3])
            Cq = al(None, "Cq", [128, 8, 128], BF16)
            EG = al(None, "EGC", [128, 8, 128])
            Ckv = EG.rearrange("p a b -> p (a b)").bitcast(BF16).rearrange("p (a b) -> p a b", a=16)
            CWC = al(None, "CWC", [128, 24, 4])
            SM = al(None, "SM", [128, 16, 8])
            DTB = al(None, "DTB", [128, 8])
            NEGA = al(None, "NEGA", [128, 8])
            ONR = al(None, "ONR", [128, 2])
            ONESB = al(None, "ONESB", [128, 2], BF16)
            SQZ = al(None, "SQZ", [128, 1024])
            SQ = SQZ.bitcast(BF16).rearrange("p (a b) -> p a b", a=16)
            ZS = SQZ
            GO = al(None, "GO", [128, 8, 128])
            KN = al(None, "KN", [128, 8, 128], BF16)
            VB = al(None, "VB", [128, 8, 128], BF16)
            KNT = al(None, "KNT", [128, 8, 128], BF16)
            RS = [al(None, "RS%d" % k, [128, 2, 131]) for k in range(2)]
            ACC = [al(None, "ACC%d" % k, [128, 2, 128]) for k in range(2)]
            EGL = al(None, "EGL", [128, 8, 2])
            B_egl = Buf("egl")
            D1 = al(None, "D1", [128, 4, 128])
            D2 = al(None, "D2", [128, 4, 128])
            Xp_w, XT_w, PTm_w = (NEU[:, k, :].rearrange("p (a b) -> p a b", a=4) for k in range(3))
            Xp, XT, PTm = (a.bitcast(F32) for a in (Xp_w, XT_w, PTm_w))
            SOLV = al(None, "SOLV", [128, 4, 128])
            PTB = al(None, "PTB", [128, 4, 128], BF16)
            QKDT = al(None, "QKDT", [128, 4, 128], BF16)
            RHSK = al(None, "RHSK", [128, 4, 128], BF16)
            SKT = al(None, "SKT", [128, 4, 128], BF16)
            QST = al(None, "QST", [128, 4, 128], BF16)
            KDEC = al(None, "KDEC", [128, 4, 128], BF16)
            UU = al(None, "UU", [128, 4, 128], BF16)
            OGZ = al(None, "OGZ", [128, D], BF16)
            B_ogz = Buf("ogz")
            NM = 128
            B_hg = Buf("hgscratch")
            B_s, B_sb, B_rh, B_cq, B_ckv, B_cwc, B_sm, B_par, B_sqz, B_go, B_kn, B_vb, B_knt = (
                Buf(n) for n in ("s", "sb", "rh", "cq", "ckv", "cwc", "sm", "par", "sqz", "go", "kn", "vb", "knt"))
            B_rs = [Buf("rs0"), Buf("rs1")]
            B_acc = [Buf("acc0"), Buf("acc1")]
            B_d1, B_d2, B_xp, B_xt, B_ptm, B_solv, B_ptb, B_qkdt, B_rhsk, B_skt, B_qst, B_kdec, B_uu = (
                Buf(n) for n in ("d1", "d2", "xp", "xt", "ptm", "solv", "ptb", "qkdt", "rhsk", "skt", "qst", "kdec", "uu"))
            hg_bufs = [B_d1, B_d2, B_xp, B_xt, B_ptm, B_solv, B_ptb, B_qkdt, B_rhsk, B_skt, B_qst, B_kdec, B_uu]
            conv_bufs = B_rs + B_acc
            load_gp(layer, GO.rearrange("p a b -> p (a b)"), [B_go])
            if not prompt:
                for k3 in range(3):
                    P.op("dve", I("tensor_scalar", out=NEU[:, k3, :], in0=GP[:, 0:512], scalar1=0.0, scalar2=None,
                                  op0=ALU.mult), reads=[B_gp], writes=[B_xp, B_xt, B_ptm])
            c_G, c_BETA, c_GC, c_DL, c_BE, c_RQ, c_RK, c_OSS, c_FAC, c_T1, c_GA, c_NGC = range(12)
            IDF = CONST[:, 0, :]
            ONESF = CONST[:, 1, :]
            MLS = CONST[:, 3, :]
            MUI = CONST[:, 4, :]

            for j in range(4):
                src = bass.AP(conv_w_c.tensor, (i * 4 + j) * 3072, [[1, 128], [128, 24]])
                P.dma(I("dma_start", out=CWC[:, :, j], in_=src, allow_slow_non_contiguous=True), writes=[B_cwc])
            P.dma(I("dma_start", out=DTB[:], in_=dt_bias[i:i + 1, :].to_broadcast([128, 8])), writes=[B_par])
            P.dma(I("dma_start", out=NEGA[:], in_=a_log[i:i + 1, :].to_broadcast([128, 8])), writes=[B_par])
            onr_src = bass.AP(out_norm.tensor, i * 128, [[1, 128], [1, 1]])
            P.dma(I("dma_start", out=ONR[:, 0:1], in_=onr_src), writes=[B_par])
            P.op("dve", I("tensor_scalar", out=WOUT[:].rearrange("p a b -> p (a b)"),
                          in0=WOUT[:].rearrange("p a b -> p (a b)"), scalar1=ONR[:, 0:1], scalar2=None, op0=ALU.mult),
                 reads=[B_wout, B_par], writes=[B_wout])
            P.op("act", I("activation", out=NEGA[:], in_=NEGA[:], func=AF.Exp), reads=[B_par], writes=[B_par])
            P.op("dve", I("tensor_scalar", out=NEGA[:], in0=NEGA[:], scalar1=-1.0, scalar2=None, op0=ALU.mult),
                 reads=[B_par], writes=[B_par])
            P.op("pool", I("memset", ONESB[:], 1.0), writes=[B_par])
            P.op("pool", I("memset", UU[:], 0.0), writes=[B_uu])
            if prompt:
                P.op("pool", I("memset", S[:], 0.0), writes=[B_s])
                P.op("pool", I("memset", SB[:], 0.0), writes=[B_sb])
                P.op("pool", I("memset", RH[:], 0.0), writes=[B_rh])
            stage(12)

            for t in range(ntiles):
                seq = job if prompt else t
                if not prompt:
                    for j in range(3):
                        src = bass.AP(scc.tensor, ((i * NSS + t) * 3 + j) * 3072, [[1, 128], [128, 24]])
                        P.dma(I("dma_start", out=RH[:, :, j], in_=src, allow_slow_non_contiguous=True), writes=[B_rh])
                    ssrc = bass.AP(ssc.tensor, (i * NSS + t) * 8 * 128 * 128, [[128, 128], [16384, 8], [1, 128]])
                    P.dma(I("dma_start", out=S[:], in_=ssrc), writes=[B_s])
                    P.op("act", I("copy", out=SB[:], in_=S[:]), reads=[B_s], writes=[B_sb])
                pre_norm(t, NT, layer)
                stage(13)
                for g in range(6):
                    bv, bb = fm_group(g * 512, NT, g, bank_fn=next_front)
                    for hf in range(2):
                        k2 = (2 * g + hf) % 2
                        rs, brs = RS[k2], B_rs[k2]
                        acc, bacc = ACC[k2], B_acc[k2]
                        f0 = g * 4 + hf * 2
                        P.op("pool", I("tensor_copy", out=rs[:, :, 0:3], in_=RH[:, f0:f0 + 2, :]),
                             reads=[B_rh], writes=[brs])
                        P.op("act", I("copy", out=rs[:, :, 3:3 + NT], in_=bv[:, hf * 2:hf * 2 + 2, 0:NT]),
                             reads=[bb], writes=[brs])
                        P.op("pool", I("tensor_copy", out=RH[:, f0:f0 + 2, :], in_=rs[:, :, NT:NT + 3]),
                             reads=[brs], writes=[B_rh])
                        for f2 in range(2):
                            ft = f0 + f2
                            P.op("act", I("activation", out=acc[:, f2, 0:NT], in_=rs[:, f2, 0:NT], func=AF.Identity,
                                          scale=CWC[:, ft, 0:1]), reads=[brs, B_cwc], writes=[bacc])
                            for j in (1, 2, 3):
                                P.op("dve", I("scalar_tensor_tensor", out=acc[:, f2, 0:NT], in0=rs[:, f2, j:j + NT],
                                              scalar=CWC[:, ft, j:j + 1], in1=acc[:, f2, 0:NT],
                                              op0=ALU.mult, op1=ALU.add), reads=[brs, B_cwc, bacc], writes=[bacc])
                        if f0 < 8:
                            dst, bd = Cq[:, f0:f0 + 2, 0:NT], B_cq
                        else:
                            dst, bd = Ckv[:, f0 - 8:f0 - 6, 0:NT], B_ckv
                        P.op("act", I("activation", out=dst, in_=acc[:, :, 0:NT], func=AF.Silu), reads=[bacc], writes=[bd])
                if t == ntiles - 1 or not prompt:
                    for j in range(3):
                        if prompt:
                            dst = bass.AP(o_cc_p.tensor, ((i * NPS + job) * 3 + j) * 3072, [[1, 128], [128, 24]])
                        else:
                            dst = bass.AP(o_cc_s.tensor, ((i * NSS + t) * 3 + j) * 3072, [[1, 128], [128, 24]])
                        P.dma(I("dma_start", out=dst, in_=RH[:, :, j], allow_slow_non_contiguous=True), reads=[B_rh])
                stage(14)
                bank, bb = tm_group(4096, 16, NT, 8, bank_fn=next_back)
                P.op("dve", I("tensor_tensor", out=SM[0:NT, c_GA, :], in0=bank[0:NT, 8:16], in1=DTB[0:NT, :], op=ALU.add),
                     reads=[bb, B_par], writes=[B_sm])
                P.op("act", I("activation", out=SM[0:NT, c_BETA, :], in_=bank[0:NT, 0:8], func=AF.Exp, scale=-1.0),
                     reads=[bb], writes=[B_sm])
                P.op("act", I("activation", out=SM[0:NT, c_BETA, :], in_=SM[0:NT, c_BETA, :], func=AF.Ln,
                              bias=CONST[0:NT, 1, 0:1], scale=1.0), reads=[B_sm, B_const], writes=[B_sm])
                P.op("act", I("activation", out=SM[0:NT, c_BETA, :], in_=SM[0:NT, c_BETA, :], func=AF.Exp, scale=-1.0),
                     reads=[B_sm], writes=[B_sm])
                P.op("act", I("activation", out=SM[0:NT, c_GA, :], in_=SM[0:NT, c_GA, :], func=AF.Exp),
                     reads=[B_sm], writes=[B_sm])
                P.op("act", I("activation", out=SM[0:NT, c_GA, :], in_=SM[0:NT, c_GA, :], func=AF.Ln,
                              bias=CONST[0:NT, 1, 0:1], scale=1.0), reads=[B_sm, B_const], writes=[B_sm])
                P.op("dve", I("tensor_tensor", out=SM[0:NT, c_G, :], in0=SM[0:NT, c_GA, :], in1=NEGA[0:NT, :], op=ALU.mult),
                     reads=[B_sm, B_par], writes=[B_sm])
                P.op("act", I("activation", out=SQ[:, 0:8, 0:NT], in_=Cq[:, :, 0:NT], func=AF.Square),
                     reads=[B_cq], writes=[B_sqz])
                P.op("act", I("activation", out=SQ[:, 8:16, 0:NT], in_=Ckv[:, 0:8, 0:NT], func=AF.Square),
                     reads=[B_ckv], writes=[B_sqz])
                bank, bb = next_back()
                P.op("pe", [I("matmul", bank[0:NT, 2 * j:2 * j + 2], lhsT=SQ[:, j, 0:NT], rhs=ONESB[:, 0:2],
                              start=True, stop=True) for j in range(16)], reads=[B_sqz, B_par], writes=[bb])
                bkv = bank[0:NT, 0:32].rearrange("p (a b) -> p a b", b=2)
                P.op("act", I("activation", out=SM[0:NT, c_RQ:c_RQ + 2, :].rearrange("p a b -> p (a b)"),
                              in_=bkv[:, :, 0], func=AF.Ln, bias=EPSB[0:NT, :], scale=1.0),
                     reads=[bb, B_eps], writes=[B_sm])
                P.op("act", I("activation", out=SM[0:NT, c_RQ:c_RQ + 2, :], in_=SM[0:NT, c_RQ:c_RQ + 2, :],
                              func=AF.Exp, scale=-0.5), reads=[B_sm], writes=[B_sm])
                for zb in range(2):
                    bank, bb = tm_group(3072 + zb * 512, 512, NT, 6 + zb, bank_fn=next_back)
                    P.op("act", I("activation", out=ZS[0:NT, zb * 512:(zb + 1) * 512], in_=bank[0:NT, :], func=AF.Silu),
                         reads=[bb], writes=[B_sqz])
                stage(15)
                pTv = pT[:].rearrange("p (a b) -> p a b", a=8)
                P.op("pe", [I("transpose", pTv[0:NT, h, :], Ckv[:, h, 0:NT], IDB[:, :]) for h in range(8)],
                     reads=[B_ckv, B_idb], writes=[B_pT])
                P.op("dve", I("tensor_tensor", out=KN[0:NT, :, :], in0=pTv[0:NT, :, :],
                              in1=SM[0:NT, c_RK, :].unsqueeze(2).to_broadcast([NT, 8, 128]), op=ALU.mult),
                     reads=[B_pT, B_sm], writes=[B_kn])
                P.op("pe", [I("transpose", pTv[0:NT, h, :], Ckv[:, 8 + h, 0:NT], IDB[:, :]) for h in range(8)],
                     reads=[B_ckv, B_idb], writes=[B_pT])
                P.op("dve", I("tensor_tensor", out=VB[0:NT, :, :], in0=pTv[0:NT, :, :],
                              in1=SM[0:NT, c_BETA, :].unsqueeze(2).to_broadcast([NT, 8, 128]), op=ALU.mult),
                     reads=[B_pT, B_sm], writes=[B_vb])
                P.op("pe", [I("transpose", pTv[:, h, 0:NT], KN[0:NT, h, :], IDB[0:NT, 0:NT]) for h in range(8)],
                     reads=[B_kn, B_idb], writes=[B_pT])
                P.op("act", I("copy", out=KNT[:, :, 0:NT], in_=pTv[:, :, 0:NT]), reads=[B_pT], writes=[B_knt])
                stage(16)
                bank, bb = next_back()
                P.op("pe", I("matmul", bank[0:NT, 0:8], lhsT=MUI[0:NT, 0:NT], rhs=SM[0:NT, c_G, :], start=True, stop=True),
                     reads=[B_const, B_sm], writes=[bb])
                P.op("dve", I("tensor_copy", out=SM[0:NT, c_GC, :], in_=bank[0:NT, 0:8]), reads=[bb], writes=[B_sm])
                P.op("dve", I("tensor_scalar", out=SM[0:NT, c_NGC, :], in0=bank[0:NT, 0:8], scalar1=-1.0, scalar2=None,
                              op0=ALU.mult), reads=[bb], writes=[B_sm])
                GSEL = GO
                P.op("dve", I("tensor_tensor", out=GSEL[0:NT, :, 0:NT],
                              in0=MUI[0:NT, 0:NT].unsqueeze(1).to_broadcast([NT, 8, NT]),
                              in1=SM[0:NT, c_G, :].unsqueeze(2).to_broadcast([NT, 8, NT]), op=ALU.mult),
                     reads=[B_const, B_sm], writes=[B_go])
                P.op("act", I("activation", out=SM[0:NT, c_BE, :], in_=SM[0:NT, c_GC, :], func=AF.Exp),
                     reads=[B_sm], writes=[B_sm])
                P.op("dve", I("tensor_tensor", out=SM[0:NT, c_BE, :], in0=SM[0:NT, c_BE, :], in1=SM[0:NT, c_BETA, :],
                              op=ALU.mult), reads=[B_sm], writes=[B_sm])
                stage(17)
                for hg in range(2):
                    hs = slice(hg * 4, hg * 4 + 4)
                    GRW = pB[:, 0:4 * NT].rearrange("p (h m) -> p h m", h=4)
                    P.op("pe", I("matmul", GRW[:, :, :], lhsT=ONESF[0:NT, :], rhs=GSEL[0:NT, hs, 0:NT],
                                 start=True, stop=True), reads=[B_const, B_go], writes=[B_pB])
                    P.op("act", I("activation", out=EG[:, hs, 0:NT], in_=GRW[:, :, :], func=AF.Exp),
                         reads=[B_pB, B_kn, B_vb], writes=[B_ckv])
                    for c in range(nch):
                        P.op("act", I("copy", out=EGL[:, hs, c], in_=EG[:, hs, 64 * c + 63]),
                             reads=[B_ckv], writes=[B_egl])
                    for c in range(nch):
                        r0 = 64 * c
                        P.op("dve", I("tensor_tensor", out=SM[r0:r0 + 64, c_DL, hs], in0=GRW[r0:r0 + 64, :, r0 + 63],
                                      in1=SM[r0:r0 + 64, c_GC, hs], op=ALU.subtract), reads=[B_pB, B_sm], writes=[B_sm])
                    P.op("act", I("activation", out=SM[0:NT, c_DL, hs], in_=SM[0:NT, c_DL, hs], func=AF.Exp),
                         reads=[B_sm], writes=[B_sm])
                    P.op("dve", I("tensor_tensor", out=D2[0:NT, :, 0:NT], in0=GRW[0:NT, :, :],
                                  in1=SM[0:NT, c_GC, hs].unsqueeze(2).to_broadcast([NT, 4, NT]), op=ALU.subtract),
                         reads=[B_pB, B_sm], writes=[B_d2])
                    P.op("dve", I("tensor_scalar", out=D1[0:NT, :, 0:NT], in0=D2[0:NT, :, 0:NT], scalar1=0.0,
                                  scalar2=-1.0, op0=ALU.max, op1=ALU.mult), reads=[B_d2], writes=[B_d1])
                    P.op("dve", I("tensor_scalar", out=D2[0:NT, :, 0:NT], in0=D2[0:NT, :, 0:NT], scalar1=0.0,
                                  scalar2=None, op0=ALU.min), reads=[B_d2], writes=[B_d2])
                    P.op("act", I("activation", out=D1[0:NT, :, 0:NT], in_=D1[0:NT, :, 0:NT], func=AF.Exp),
                         reads=[B_d1], writes=[B_d1])
                    P.op("act", I("activation", out=D2[0:NT, :, 0:NT], in_=D2[0:NT, :, 0:NT], func=AF.Exp),
                         reads=[B_d2], writes=[B_d2])
                    P.op("dve", I("tensor_tensor", out=D1[0:NT, :, 0:NT], in0=D1[0:NT, :, 0:NT],
                                  in1=MLS[0:NT, 0:NT].unsqueeze(1).to_broadcast([NT, 4, NT]), op=ALU.mult),
                         reads=[B_d1, B_const], writes=[B_d1])
                    P.op("dve", I("tensor_tensor", out=D2[0:NT, :, 0:NT], in0=D2[0:NT, :, 0:NT],
                                  in1=MUI[0:NT, 0:NT].unsqueeze(1).to_broadcast([NT, 4, NT]), op=ALU.mult),
                         reads=[B_d2, B_const], writes=[B_d2])
                    bank, bb = next_back()
                    bkv = bank[:].rearrange("p (a b) -> p a b", a=4)
                    P.op("pe", [I("matmul", bkv[0:NT, hl, 0:NT], lhsT=KNT[:, hg * 4 + hl, 0:NT],
                                  rhs=KNT[:, hg * 4 + hl, 0:NT], start=True, stop=True) for hl in range(4)],
                         reads=[B_knt], writes=[bb])
                    for hl in range(4):
                        P.op("dve", I("scalar_tensor_tensor", out=Xp_w[0:NT, hl, 0:NT], in0=bkv[0:NT, hl, 0:NT],
                                      scalar=SM[0:NT, c_BETA, hg * 4 + hl:hg * 4 + hl + 1], in1=D1[0:NT, hl, 0:NT],
                                      op0=ALU.mult, op1=ALU.mult), reads=[bb, B_d1, B_sm], writes=[B_xp])
                    bank, bb = next_back()
                    bkv = bank[:].rearrange("p (a b) -> p a b", a=4)
                    P.op("pe", [I("matmul", bkv[0:NT, hl, 0:NT], lhsT=KNT[:, hg * 4 + hl, 0:NT],
                                  rhs=Cq[:, hg * 4 + hl, 0:NT], start=True, stop=True) for hl in range(4)],
                         reads=[B_knt, B_cq], writes=[bb])
                    P.op("dve", I("tensor_tensor", out=QKDT[0:NT, :, 0:NT], in0=bkv[0:NT, :, 0:NT],
                                  in1=D2[0:NT, :, 0:NT], op=ALU.mult), reads=[bb, B_d2], writes=[B_qkdt])
                    bank, bb = next_back()
                    bkv = bank[:].rearrange("p (a b) -> p a b", a=4)
                    P.op("pe", [I("transpose", bkv[0:NT, hl, 0:NT], Xp[0:NT, hl, 0:NT], IDF[0:NT, 0:NT])
                                for hl in range(4)], reads=[B_xp, B_const], writes=[bb])
                    P.op("act", I("copy", out=XT_w[0:NT, :, 0:NT], in_=bkv[0:NT, :, 0:NT]), reads=[bb], writes=[B_xt])
                    P.op("dve", I("tensor_tensor", out=PTm_w[0:NT, :, 0:NT],
                                  in0=IDF[0:NT, 0:NT].unsqueeze(1).to_broadcast([NT, 4, NT]),
                                  in1=bkv[0:NT, :, 0:NT], op=ALU.subtract), reads=[bb, B_const], writes=[B_ptm])
                    for lev in range(1, 6):
                        b1, bb1 = next_back()
                        b1v = b1[:].rearrange("p (a b) -> p a b", a=4)
                        P.op("pe", [I("matmul", b1v[:, hl, :], lhsT=XT_w[:, hl, :], rhs=Xp_w[:, hl, :],
                                      start=True, stop=True) for hl in range(4)], reads=[B_xt, B_xp], writes=[bb1])
                        if lev < 5:
                            b2, bb2 = next_back()
                            b2v = b2[:].rearrange("p (a b) -> p a b", a=4)
                            P.op("pe", [I("matmul", b2v[:, hl, :], lhsT=Xp_w[:, hl, :],
                                          rhs=XT_w[:, hl, :], start=True, stop=True) for hl in range(4)],
                                 reads=[B_xt, B_xp], writes=[bb2])
                        P.op("act", I("copy", out=Xp_w[0:NT, :, 0:NT], in_=b1v[0:NT, :, 0:NT]), reads=[bb1], writes=[B_xp])
                        if lev < 5:
                            P.op("dve", I("tensor_copy", out=XT_w[0:NT, :, 0:NT], in_=b2v[0:NT, :, 0:NT]),
                                 reads=[bb2], writes=[B_xt])
                        b3, bb3 = next_back()
                        b3v = b3[:].rearrange("p (a b) -> p a b", a=4)
                        P.op("pe", [I("matmul", b3v[:, hl, :], lhsT=Xp_w[:, hl, :], rhs=PTm_w[:, hl, :],
                                      start=True, stop=True) for hl in range(4)], reads=[B_xp, B_ptm], writes=[bb3])
                        P.op("dve", I("tensor_tensor", out=PTm_w[0:NT, :, 0:NT], in0=PTm[0:NT, :, 0:NT],
                                      in1=b3v[0:NT, :, 0:NT], op=ALU.add), reads=[bb3, B_ptm], writes=[B_ptm])
                    P.op("act", I("copy", out=PTB[0:NT, :, 0:NT], in_=PTm[0:NT, :, 0:NT]), reads=[B_ptm], writes=[B_ptb])
                    P.op("dve", I("tensor_tensor", out=RHSK[0:NT, :, :], in0=KN[0:NT, hs, :],
                                  in1=SM[0:NT, c_BE, hs].unsqueeze(2).to_broadcast([NT, 4, 128]), op=ALU.mult),
                         reads=[B_kn, B_sm], writes=[B_rhsk])
                    P.op("dve", I("tensor_tensor", out=KDEC[0:NT, :, :], in0=KN[0:NT, hs, :],
                                   in1=SM[0:NT, c_DL, hs].unsqueeze(2).to_broadcast([NT, 4, 128]), op=ALU.mult),
                         reads=[B_kn, B_sm], writes=[B_kdec])
                    P.op("dve", I("tensor_tensor", out=QST[:, :, 0:NT], in0=Cq[:, hs, 0:NT], in1=EG[:, hs, 0:NT],
                                  op=ALU.mult), reads=[B_cq, B_ckv], writes=[B_qst])
                    bank, bb = next_back()
                    bkv = bank[:].rearrange("p (a b) -> p a b", a=4)
                    P.op("pe", [I("matmul", bkv[:, hl, 0:NT], lhsT=RHSK[0:NT, hl, :], rhs=PTB[0:NT, hl, 0:NT],
                                  start=True, stop=True) for hl in range(4)], reads=[B_rhsk, B_ptb], writes=[bb])
                    P.op("act", I("copy", out=SKT[:, :, 0:NT], in_=bkv[:, :, 0:NT]), reads=[bb], writes=[B_skt])
                    bank, bb = next_back()
                    bkv = bank[:].rearrange("p (a b) -> p a b", a=4)
                    P.op("pe", [I("matmul", bkv[0:NT, hl, :], lhsT=PTB[0:NT, hl, 0:NT], rhs=VB[0:NT, hg * 4 + hl, :],
                                  start=True, stop=True) for hl in range(4)], reads=[B_vb, B_ptb], writes=[bb])
                    P.op("act", I("copy", out=SOLV[0:NT, :, :], in_=bkv[0:NT, :, :]), reads=[bb], writes=[B_solv])
                    OB = pA[:, 0:512].rearrange("p (a b) -> p a b", a=4)
                    for c in range(nch):
                        r0 = 64 * c
                        bw, bbw = next_back()
                        bwv = bw[:].rearrange("p (a b) -> p a b", a=4)
                        P.op("pe", [I("matmul", bwv[r0:r0 + 64, hl, :], lhsT=SKT[:, hl, r0:r0 + 64],
                                      rhs=SB[:, hg * 4 + hl, :], start=True, stop=True) for hl in range(4)],
                             reads=[B_skt, B_sb], writes=[bbw])
                        P.op("dve", I("tensor_tensor", out=UU[r0:r0 + 64, :, :], in0=SOLV[r0:r0 + 64, :, :],
                                      in1=bwv[r0:r0 + 64, :, :], op=ALU.subtract), reads=[bbw, B_solv], writes=[B_uu])
                        fns = []
                        for hl in range(4):
                            fns.append(I("matmul", OB[r0:r0 + 64, hl, :], lhsT=QST[:, hl, r0:r0 + 64],
                                         rhs=SB[:, hg * 4 + hl, :], start=True, stop=False))
                            fns.append(I("matmul", OB[r0:r0 + 64, hl, :], lhsT=QKDT[0:NT, hl, r0:r0 + 64],
                                         rhs=UU[0:NT, hl, :], start=False, stop=True))
                        P.op("pe", fns, reads=[B_qst, B_sb, B_qkdt, B_uu], writes=[B_pA])
                        bs, bbs = next_back()
                        bsv = bs[:].rearrange("p (a b) -> p a b", a=4)
                        P.op("pe", [I("matmul", bsv[:, hl, :], lhsT=KDEC[r0:r0 + 64, hl, :], rhs=UU[r0:r0 + 64, hl, :],
                                      start=True, stop=True) for hl in range(4)], reads=[B_kdec, B_uu], writes=[bbs])
                        for hl in range(4):
                            h = hg * 4 + hl
                            P.op("dve", I("scalar_tensor_tensor", out=S[:, h, :], in0=S[:, h, :],
                                          scalar=EGL[:, h, c:c + 1], in1=bsv[:, hl, :],
                                          op0=ALU.mult, op1=ALU.add), reads=[bbs, B_s, B_egl], writes=[B_s])
                        P.op("act", I("copy", out=SB[:, hs, :], in_=S[:, hs, :]), reads=[B_s], writes=[B_sb])
                    for hl in range(4):
                        h = hg * 4 + hl
                        P.op("act", I("activation", out=OGZ[0:NT, 0:128], in_=OB[0:NT, hl, :], func=AF.Square,
                                      accum_out=SM[0:NT, c_OSS, h:h + 1]), reads=[B_pA], writes=[B_ogz, B_sm])
                    P.op("dve", I("tensor_tensor", out=SM[0:NT, c_T1, hs], in0=SM[0:NT, c_RQ, hs], in1=SM[0:NT, c_RQ, hs],
                                  op=ALU.mult), reads=[B_sm], writes=[B_sm])
                    P.op("dve", I("tensor_tensor", out=SM[0:NT, c_T1, hs], in0=SM[0:NT, c_T1, hs], in1=SM[0:NT, c_OSS, hs],
                                  op=ALU.mult), reads=[B_sm], writes=[B_sm])
                    P.op("act", I("activation", out=SM[0:NT, c_T1, hs], in_=SM[0:NT, c_T1, hs], func=AF.Ln,
                                  bias=EPSB[0:NT, :], scale=1.0 / (128.0 * 128.0)), reads=[B_sm, B_eps], writes=[B_sm])
                    P.op("act", I("activation", out=SM[0:NT, c_T1, hs], in_=SM[0:NT, c_T1, hs], func=AF.Exp, scale=-0.5),
                         reads=[B_sm], writes=[B_sm])
                    P.op("dve", I("scalar_tensor_tensor", out=SM[0:NT, c_FAC, hs], in0=SM[0:NT, c_RQ, hs],
                                  scalar=128.0 ** -0.5, in1=SM[0:NT, c_T1, hs], op0=ALU.mult, op1=ALU.mult),
                         reads=[B_sm], writes=[B_sm])
                    OG = GO
                    P.op("dve", I("tensor_tensor", out=OG[0:NT, hs, :], in0=OB[0:NT, :, :],
                                  in1=SM[0:NT, c_FAC, hs].unsqueeze(2).to_broadcast([NT, 4, 128]), op=ALU.mult),
                         reads=[B_pA, B_sm], writes=[B_go])
                stage(18)
                P.op("dve", I("tensor_tensor", out=OGZ[0:NT, :], in0=GO[0:NT, :, :].rearrange("p a b -> p (a b)"),
                              in1=ZS[0:NT, :], op=ALU.mult), reads=[B_go, B_sqz], writes=[B_ogz])
                gbank, gbb = next_back()
                gTv = gbank.bitcast(BF16).rearrange("p (a b) -> p a b", a=8)
                P.op("pe", [I("transpose", gTv[:, kc, 0:NT], OGZ[0:NT, kc * 128:(kc + 1) * 128], IDB[0:NT, 0:NT])
                            for kc in range(8)], reads=[B_ogz, B_idb], writes=[gbb])
                P.op("act", I("copy", out=uT[:, :, 0:NT], in_=gTv[:, :, 0:NT]), reads=[gbb], writes=[B_uT])
                stage(19)
                out_proj_post(t, NT, layer, last, uT, B_uT, ZS.bitcast(BF16)[:, 0:D], B_sqz, pB, BF_pB)
                if last:
                    store_x(job, t, NT)
                if t == ntiles - 1 or not prompt:
                    if prompt:
                        dst = bass.AP(o_sc_p.tensor, (i * NPS + job) * 8 * 16384, [[128, 128], [16384, 8], [1, 128]])
                    else:
                        dst = bass.AP(o_sc_s.tensor, (i * NSS + t) * 8 * 16384, [[128, 128], [16384, 8], [1, 128]])
                    P.dma(I("dma_start", out=dst, in_=S[:]), reads=[B_s])
                stage(20)
            P.barrier()

        try:
            stage(1)
            for job in jobs:
                load_x(job)
                P.new_phase()
                for layer in range(n_layers):
                    last = layer == n_layers - 1
                    if layer % 2 == 0:
                        even_layer(job, layer, last)
                    else:
                        odd_layer(job, layer, last)
        except _Stop:
            pass
        P.barrier()
        if os.environ.get("KVERBOSE"):
            print("sem counts", {str(k): v for k, v in P.cnt.items() if v > 2000}, "nsems", len(P.sems), "nops", len(P.recs), "nfill", P.nfill)
        P.replay()
    return nc


def make_consts():
    c = np.zeros((128, 5, 128), np.float32)
    idx = np.arange(128)
    same = (idx[:, None] // 64) == (idx[None, :] // 64)
    c[:, 0, :] = np.eye(128, dtype=np.float32)
    c[:, 1, :] = 1.0
    c[:, 2, :] = (same & (idx[None, :] <= idx[:, None])).astype(np.float32)
    c[:, 3, :] = (same & (idx[None, :] < idx[:, None])).astype(np.float32)
    c[:, 4, :] = (same & (idx[None, :] >= idx[:, None])).astype(np.float32)
    return c


def kernel(x_prompt, x_sample, cache_conv_a, cache_k_b, cache_v_b, state_conv_c, state_s_c,
           norm_pre, norm_post, w_in_even, conv_w_a, rel_bias_b, w_out_even,
           w_in_odd, conv_w_c, a_log_c, dt_bias_c, out_norm_c, w_out_odd, _n_layers=int(os.environ.get("KNL", "4")), _jobs=(0, 1, 2), _cores=NCORES):
    f = lambda a: np.ascontiguousarray(np.asarray(a, dtype=np.float32))
    consts = make_consts()
    shared = dict(norm_pre=f(norm_pre), norm_post=f(norm_post), w_in_even=f(w_in_even), conv_w_a=f(conv_w_a),
                  rel_bias=f(rel_bias_b), w_out_even=f(w_out_even), w_in_odd=f(w_in_odd), conv_w_c=f(conv_w_c),
                  a_log=f(a_log_c), dt_bias=f(dt_bias_c), out_norm=f(out_norm_c), w_out_odd=f(w_out_odd),
                  consts=consts)
    in_maps = []
    for c in range(_cores):
        ps_, ss_ = slice(NPS * c, NPS * (c + 1)), slice(NSS * c, NSS * (c + 1))
        m = dict(shared)
        m["xp"] = f(x_prompt[ps_])
        m["xs"] = f(x_sample[ss_])
        m["cca"] = f(cache_conv_a[:, ss_])
        m["ckb"] = f(cache_k_b[:, ss_]).reshape(2, NSS, 512, 512)
        m["cvb"] = f(cache_v_b[:, ss_]).reshape(2, NSS, 512, 512)
        m["scc"] = f(state_conv_c[:, ss_])
        m["ssc"] = f(state_s_c[:, ss_])
        in_maps.append(m)
    nc = build_program(_n_layers, _jobs)
    res = run_bass_kernel_spmd(nc, in_maps, core_ids=list(range(_cores)))
    R = res.results
    cat0 = lambda k: np.concatenate([r[k] for r in R], axis=0)
    cat1 = lambda k: np.concatenate([r[k] for r in R], axis=1)
    yp = cat0("o_yp")
    ys = cat0("o_ys")
    ca_p = cat1("o_ca_p")
    kb_p = cat1("o_kb_p").reshape(2, -1, 512, 8, 64)
    vb_p = cat1("o_vb_p").reshape(2, -1, 512, 8, 64)
    cc_p = cat1("o_cc_p")
    sc_p = cat1("o_sc_p")
    ca_s = cat1("o_ca_s")
    kb_s = cat1("o_kb_s").reshape(2, -1, DEC, 8, 64)
    vb_s = cat1("o_vb_s").reshape(2, -1, DEC, 8, 64)
    cc_s = cat1("o_cc_s")
    sc_s = cat1("o_sc_s")
    return (yp, ys, ca_p, kb_p, vb_p, cc_p, sc_p, ca_s, kb_s, vb_s, cc_s, sc_s)
```
